# Optimizing a Trainium2 kernel written in Bass

```python
import math
import jax, jax.numpy as jnp
from jax import lax
import numpy as np


D_MODEL = 1024
BATCH = 16
SEQ = 2048
DEPTH = 1

D_RNN = D_MODEL
N_RNN_BLOCKS = 16
RNN_BLOCK = D_RNN // N_RNN_BLOCKS
RNN_CONV_W = 4
RNN_CONV_LEFT = 2
RG_C = 8.0
N_DIR = 2
N_HEADS = 8
HEAD_DIM = 64
V_DIM = 2 * HEAD_DIM
ATTN_WIDTH = N_HEADS * V_DIM
QK_WIDTH = N_HEADS * 2 * HEAD_DIM
ROPE_DIM = HEAD_DIM // 4
ROPE_THETA = 500000.0
Q_BLOCK = 128
N_BRANCHES = 2
D_FF = 2816
FFN_CONV_W = 3
FFN_CONV_LEFT = 1
N_MOD = 6
NORM_EPS = 1e-6

IN_SIZES = (D_RNN, D_RNN, QK_WIDTH, QK_WIDTH, ATTN_WIDTH, N_BRANCHES * D_MODEL)
IN_COLS = sum(IN_SIZES)
IN_SPLITS = tuple(int(v) for v in np.cumsum(IN_SIZES)[:-1])

kernel_name = "hybrid_rglru_diffattn_convffn_encoder_block"


def lambda_init_fn(layer_idx):
    return 0.8 - 0.6 * math.exp(-0.3 * layer_idx)


def rms_norm(x, g):
    xf = x.astype(jnp.float32)
    y = xf * lax.rsqrt(jnp.mean(xf * xf, axis=-1, keepdims=True) + NORM_EPS)
    return (y * g.astype(jnp.float32)).astype(x.dtype)


def depthwise_conv(x, w, b, left):
    K = w.shape[0]
    S = x.shape[1]
    xp = jnp.pad(x, ((0, 0), (left, K - 1 - left), (0, 0)))
    out = b
    for k in range(K):
        out = out + xp[:, k:k + S] * w[k]
    return out


def apply_partial_rope(t, cos, sin):
    half = ROPE_DIM // 2
    t1 = t[..., :half]
    t2 = t[..., half:ROPE_DIM]
    rot = jnp.concatenate([t1 * cos - t2 * sin, t2 * cos + t1 * sin], axis=-1)
    return jnp.concatenate([rot, t[..., ROPE_DIM:]], axis=-1)


def _lin_comb(left, right):
    a_l, b_l = left
    a_r, b_r = right
    return a_l * a_r, a_r * b_l + b_r


def bidir_rg_lru(xc, w_a, b_a, w_i, b_i, lam):
    B, S, _ = xc.shape
    xb = xc.reshape(B, S, N_RNN_BLOCKS, RNN_BLOCK)
    r = jax.nn.sigmoid(jnp.einsum('bsnk,dnkj->dbsnj', xb, w_a).reshape(N_DIR, B, S, D_RNN)
                       + b_a[:, None, None, :])
    i = jax.nn.sigmoid(jnp.einsum('bsnk,dnkj->dbsnj', xb, w_i).reshape(N_DIR, B, S, D_RNN)
                       + b_i[:, None, None, :])
    log_a = -RG_C * r.astype(jnp.float32) * jax.nn.softplus(-lam.astype(jnp.float32))[:, None, None, :]
    a = jnp.exp(log_a)
    u = jnp.sqrt(-jnp.expm1(2.0 * log_a)) * (i * xc[None]).astype(jnp.float32)
    _, h_fwd = lax.associative_scan(_lin_comb, (a[0], u[0]), axis=1)
    _, h_bwd = lax.associative_scan(_lin_comb, (a[1], u[1]), axis=1, reverse=True)
    return (h_fwd + h_bwd).astype(xc.dtype)


def diff_attention(q, k, v, lam, subln_g, lam_init):
    B, S = q.shape[0], q.shape[1]
    n_blk = S // Q_BLOCK
    scale = HEAD_DIM ** -0.5
    qb = q.reshape(B, n_blk, Q_BLOCK, N_HEADS, 2, HEAD_DIM).transpose(1, 0, 2, 3, 4, 5)

    def block(qblk):
        s = jnp.einsum('bqhcd,bkhcd->bhcqk', qblk, k).astype(jnp.float32) * scale
        p = jax.nn.softmax(s, axis=-1)
        w = p[:, :, 0] - lam * p[:, :, 1]
        return jnp.einsum('bhqk,bkhe->bqhe', w.astype(v.dtype), v)

    o = lax.map(block, qb)
    o = o.transpose(1, 0, 2, 3, 4).reshape(B, S, N_HEADS, V_DIM)
    o = rms_norm(o, subln_g) * (1.0 - lam_init)
    return o.reshape(B, S, ATTN_WIDTH)


def setup_inputs(seed: int = 0) -> dict:
    key = jax.random.key(seed)
    ks = iter(jax.random.split(key, 40))

    def nrm(shape, scale):
        return jax.random.normal(next(ks), shape, jnp.float32) * scale

    L = DEPTH
    x = nrm((BATCH, SEQ, D_MODEL), 1.0)
    c = nrm((BATCH, D_MODEL), 1.0)
    offs = jax.random.randint(next(ks), (BATCH, 1), 0, 4096, dtype=jnp.int32)
    positions = (jnp.arange(SEQ, dtype=jnp.int32)[None, :] + offs).astype(jnp.int32)

    u = jax.random.uniform(next(ks), (L, N_DIR, D_RNN), jnp.float32, 0.9, 0.999)
    base = u ** (1.0 / RG_C)
    rg_lambda = jnp.log(base / (1.0 - base))

    return {
        "x": x,
        "c": c,
        "positions": positions,
        "w_ada": nrm((L, D_MODEL, N_MOD * D_MODEL), 0.5 * D_MODEL ** -0.5),
        "b_ada": nrm((L, N_MOD * D_MODEL), 0.01),
        "norm1_g": 1.0 + nrm((L, D_MODEL), 0.02),
        "w_in": nrm((L, D_MODEL, IN_COLS), D_MODEL ** -0.5),
        "conv_rnn_w": nrm((L, RNN_CONV_W, D_RNN), RNN_CONV_W ** -0.5),
        "conv_rnn_b": nrm((L, D_RNN), 0.01),
        "w_rg_a": nrm((L, N_DIR, N_RNN_BLOCKS, RNN_BLOCK, RNN_BLOCK), RNN_BLOCK ** -0.5),
        "b_rg_a": nrm((L, N_DIR, D_RNN), 0.01),
        "w_rg_i": nrm((L, N_DIR, N_RNN_BLOCKS, RNN_BLOCK, RNN_BLOCK), RNN_BLOCK ** -0.5),
        "b_rg_i": nrm((L, N_DIR, D_RNN), 0.01),
        "rg_lambda": rg_lambda,
        "w_rnn_o": nrm((L, D_RNN, D_MODEL), D_RNN ** -0.5),
        "lam_q1": nrm((L, HEAD_DIM), 0.1),
        "lam_k1": nrm((L, HEAD_DIM), 0.1),
        "lam_q2": nrm((L, HEAD_DIM), 0.1),
        "lam_k2": nrm((L, HEAD_DIM), 0.1),
        "subln_g": 1.0 + nrm((L, V_DIM), 0.02),
        "w_attn_o": nrm((L, ATTN_WIDTH, D_MODEL), ATTN_WIDTH ** -0.5),
        "w_out": nrm((L, D_MODEL, D_MODEL), D_MODEL ** -0.5),
        "norm2_g": 1.0 + nrm((L, D_MODEL), 0.02),
        "w_up": nrm((L, D_MODEL, 2 * D_FF), D_MODEL ** -0.5),
        "conv_ffn_w": nrm((L, FFN_CONV_W, 2 * D_FF), FFN_CONV_W ** -0.5),
        "conv_ffn_b": nrm((L, 2 * D_FF), 0.01),
        "w_down": nrm((L, D_FF, D_MODEL), D_FF ** -0.5),
        "final_g": 1.0 + nrm((D_MODEL,), 0.02),
    }


def reference(x, c, positions, w_ada, b_ada, norm1_g, w_in, conv_rnn_w, conv_rnn_b,
              w_rg_a, b_rg_a, w_rg_i, b_rg_i, rg_lambda, w_rnn_o,
              lam_q1, lam_k1, lam_q2, lam_k2, subln_g, w_attn_o, w_out,
              norm2_g, w_up, conv_ffn_w, conv_ffn_b, w_down, final_g):
    B, S, _ = x.shape

    inv_freq = ROPE_THETA ** (-jnp.arange(0, ROPE_DIM, 2, dtype=jnp.float32) / ROPE_DIM)
    ang = positions.astype(jnp.float32)[..., None] * inv_freq
    cos = jnp.cos(ang)[:, :, None, None, :].astype(x.dtype)
    sin = jnp.sin(ang)[:, :, None, None, :].astype(x.dtype)
    c_act = jax.nn.silu(c)

    for l in range(DEPTH):
        lam_init = lambda_init_fn(l)
        mod = (c_act @ w_ada[l] + b_ada[l])[:, None, :]
        sh1, sc1, g1, sh2, sc2, g2 = jnp.split(mod, N_MOD, axis=-1)

        h = rms_norm(x, norm1_g[l]) * (1.0 + sc1) + sh1
        proj = h @ w_in[l]
        xr, yr, q, k, v, gates = jnp.split(proj, IN_SPLITS, axis=-1)

        xr = depthwise_conv(xr, conv_rnn_w[l], conv_rnn_b[l], RNN_CONV_LEFT)
        hr = bidir_rg_lru(xr, w_rg_a[l], b_rg_a[l], w_rg_i[l], b_rg_i[l], rg_lambda[l])
        branch_a = (hr * jax.nn.gelu(yr)) @ w_rnn_o[l]

        q = apply_partial_rope(q.reshape(B, S, N_HEADS, 2, HEAD_DIM), cos, sin)
        k = apply_partial_rope(k.reshape(B, S, N_HEADS, 2, HEAD_DIM), cos, sin)
        v = v.reshape(B, S, N_HEADS, V_DIM)
        lam = (jnp.exp(jnp.sum(lam_q1[l].astype(jnp.float32) * lam_k1[l].astype(jnp.float32)))
               - jnp.exp(jnp.sum(lam_q2[l].astype(jnp.float32) * lam_k2[l].astype(jnp.float32)))
               + lam_init)
        branch_b = diff_attention(q, k, v, lam, subln_g[l], lam_init) @ w_attn_o[l]

        gate_a, gate_b = jnp.split(jax.nn.sigmoid(gates), N_BRANCHES, axis=-1)
        merged = gate_a * branch_a + gate_b * branch_b
        x = x + g1 * (merged @ w_out[l])

        h = rms_norm(x, norm2_g[l]) * (1.0 + sc2) + sh2
        up = depthwise_conv(h @ w_up[l], conv_ffn_w[l], conv_ffn_b[l], FFN_CONV_LEFT)
        val, gt = jnp.split(up, 2, axis=-1)
        x = x + g2 * ((jax.nn.silu(gt) * val) @ w_down[l])

    return rms_norm(x, final_g)
```

```python
import numpy as np
from collections import deque
from contextlib import ExitStack
import concourse.bass as bass
import concourse.mybir as mybir
from concourse.bass_utils import run_bass_kernel_spmd

F32 = mybir.dt.float32
BF16 = mybir.dt.bfloat16
I32 = mybir.dt.int32
AF = mybir.ActivationFunctionType
ALU = mybir.AluOpType
AX = mybir.AxisListType

ENGS = ["sync", "scalar", "gpsimd", "vector", "tensor"]
STRICT = True
import os
VAR = os.environ.get("K_VAR", "")
S = 2048
NT = 16
EPS = 1e-6
PI = float(np.pi)


class Op:
    __slots__ = ("eng", "fn", "dma", "semkey", "deps", "signal", "sigval", "idx")


class Prog:
    def __init__(self, nc, ctx):
        self.nc = nc
        self.ctx = ctx
        self.ops = {e: [] for e in ENGS}
        self.last_w = {}
        self.readers = {}
        self.dma_count = {}
        self.waitall = set()
        self.outkeys = set()
        self.n = 0
        self.regmap = {}
        self.region_ops = {}
        self.pending = {}
        self.frozen = False
        self.strict = STRICT

    def declare(self, name, regions):
        self.regmap[name] = regions

    def handoff(self, region):
        self.pending[region] = list(self.region_ops.get(region, {}).values())
        self.region_ops[region] = {}

    def add(self, eng, fn, r=(), w=(), dma=False, semkey=None, waitall=False, out=False):
        if self.frozen:
            return None
        op = Op()
        op.eng, op.fn, op.dma, op.semkey = eng, fn, dma, semkey
        op.signal = False
        op.sigval = None
        op.idx = self.n
        self.n += 1
        deps = {}
        regs = set()
        for k in list(r) + list(w):
            name = k[0] if isinstance(k, tuple) else k
            for rg in self.regmap.get(name, ()):
                regs.add(rg)
        for rg in regs:
            for d in self.pending.get(rg, ()):
                deps[d] = "raw"
        for k in r:
            d = self.last_w.get(k)
            if d is not None:
                deps[d] = "raw"
        for k in w:
            d = self.last_w.get(k)
            if d is not None and d not in deps:
                deps[d] = "waw"
            lastrd = {}
            for rd in self.readers.get(k, ()):
                if rd.dma:
                    lastrd[id(rd)] = rd
                else:
                    lastrd[rd.eng] = rd
            for rd in lastrd.values():
                if rd not in deps:
                    deps[rd] = "war"
        need = []
        for d, kind in deps.items():
            if d is op:
                continue
            if d.dma:
                need.append(d)
            elif d.eng == eng and not dma:
                if eng == "tensor":
                    continue
                if kind == "raw" or self.strict:
                    need.append(d)
            else:
                need.append(d)
        for d in need:
            d.signal = True
        op.deps = need
        for k in r:
            self.readers.setdefault(k, []).append(op)
        for k in w:
            self.last_w[k] = op
            self.readers[k] = []
        if dma:
            c = self.dma_count.get(semkey, 0) + 1
            self.dma_count[semkey] = c
            op.sigval = 16 * c
            if waitall:
                self.waitall.add(semkey)
            if out:
                self.outkeys.add(semkey)
        for rg in regs:
            ro = self.region_ops.setdefault(rg, {})
            ro[("d", semkey) if dma else eng] = op
        self.ops[eng].append(op)
        return op

    def emit(self):
        nc = self.nc
        ctx = self.ctx
        esem = {e: ctx.enter_context(nc.semaphore("s_" + e)) for e in ENGS}
        dsem = {}
        for i, k in enumerate(self.dma_count):
            dsem[k] = ctx.enter_context(nc.semaphore("d%d" % i))
        for e in ENGS:
            c = 0
            for op in self.ops[e]:
                if not op.dma and op.signal:
                    c += 1
                    op.sigval = c
        self.nwaits = 0

        def emit_engine(eng_name):
            def body(eng):
                waited = {}
                for op in self.ops[eng_name]:
                    wl = {}
                    for d in op.deps:
                        if d.dma:
                            s = dsem[d.semkey]
                            v = 16 * self.dma_count[d.semkey] if d.semkey in self.waitall else d.sigval
                        else:
                            s = esem[d.eng]
                            v = d.sigval
                        key = id(s)
                        if v > wl.get(key, (None, 0))[1]:
                            wl[key] = (s, v)
                    for key, (s, v) in wl.items():
                        if waited.get(key, 0) >= v:
                            continue
                        eng.wait_ge(s, v)
                        self.nwaits += 1
                        waited[key] = v
                    ins = op.fn(eng)
                    if op.dma:
                        ins.then_inc(dsem[op.semkey], 16)
                    elif op.signal:
                        ins.then_inc(esem[eng_name], 1)
                if eng_name == "sync":
                    for k in self.outkeys:
                        eng.wait_ge(dsem[k], 16 * self.dma_count[k])
            return body

        with nc.Block() as block:
            block.sync(emit_engine("sync"))
            block.scalar(emit_engine("scalar"))
            block.gpsimd(emit_engine("gpsimd"))
            block.vector(emit_engine("vector"))
            block.tensor(emit_engine("tensor"))


def bc_last(ap, n):
    a = ap.ap
    return bass.AP(ap.tensor, ap.offset, [list(x) for x in a] + [[0, n]])


def bc_mid(ap, n):
    a = ap.ap
    return bass.AP(ap.tensor, ap.offset, [list(a[0]), [0, n]] + [list(x) for x in a[1:]])


class StopBuild(Exception):
    pass


class Builder:
    def __init__(self, debug=(), stop=None):
        self.debug = set(debug)
        self.taps = {}
        self.stop = stop

    def ck(self, name):
        if self.stop == name:
            self.P.frozen = True

    def mm(self, out, lhsT, rhs, start, stop, r, w, skip=False):
        if skip:
            self.P.add("tensor", lambda e: e.matmul(out, lhsT=lhsT, rhs=rhs, start=start, stop=stop, skip_group_check=True), r=r, w=w)
        else:
            self.P.add("tensor", lambda e: e.matmul(out, lhsT=lhsT, rhs=rhs, start=start, stop=stop), r=r, w=w)

    def tr(self, out, in_, r, w):
        ident = self.ident[:]
        self.P.add("tensor", lambda e: e.transpose(out, in_, ident), r=list(r) + ["ident"], w=w)

    def act(self, out, in_, func, r, w, bias=None, scale=1.0, accum=None):
        kw = {}
        if bias is not None:
            kw["bias"] = bias
        if accum is not None:
            kw["accum_out"] = accum
        self.P.add("scalar", lambda e: e.activation(out, in_, func, scale=scale, **kw), r=r, w=w)

    def ts(self, eng, out, in0, s1, s2, op0, op1, r, w):
        if op1 is None:
            self.P.add(eng, lambda e: e.tensor_scalar(out, in0, s1, None, op0), r=r, w=w)
        else:
            self.P.add(eng, lambda e: e.tensor_scalar(out, in0, s1, s2, op0, op1), r=r, w=w)

    def tt(self, eng, out, in0, in1, op, r, w):
        self.P.add(eng, lambda e: e.tensor_tensor(out, in0, in1, op), r=r, w=w)

    def stt(self, eng, out, in0, scalar, in1, op0, op1, r, w):
        self.P.add(eng, lambda e: e.scalar_tensor_tensor(out, in0, scalar, in1, op0, op1), r=r, w=w)

    def cp(self, eng, out, in_, r, w):
        if eng == "scalar":
            self.P.add(eng, lambda e: e.copy(out, in_), r=r, w=w)
        else:
            self.P.add(eng, lambda e: e.tensor_copy(out, in_), r=r, w=w)

    def dma(self, eng, out, in_, r, w, semkey, waitall=False, outflag=False):
        self.P.add(eng, lambda e: e.dma_start(out=out, in_=in_), r=r, w=w, dma=True, semkey=semkey,
                   waitall=waitall, out=outflag)

    def cload(self, out, in_, key, eng="sync"):
        self.dma(eng, out, in_, [], [key], "const", waitall=True)

    def tap(self, name, ap, key):
        if name not in self.debug:
            return
        shape = list(ap.shape)
        d = self.nc.dram_tensor("dbg_" + name, shape, ap.dtype, kind="ExternalOutput").ap()
        self.taps[name] = "dbg_" + name
        self.dma("sync", d, ap, list(key), [], ("dbg", name), outflag=True)

    def view(self, off, nbytes, dt, pattern=None, **kw):
        assert off % 4 == 0 and nbytes % 4 == 0
        v = self.arena[:, off // 4:(off + nbytes) // 4]
        if dt != F32:
            v = v.bitcast(dt)
        if pattern:
            v = v.rearrange(pattern, **kw)
        return v

    def cview(self, ncols, pattern=None, **kw):
        o = self.coff
        self.coff += ncols
        v = self.cst[:, o:o + ncols]
        if pattern:
            v = v.rearrange(pattern, **kw)
        return v

    def wfm_load(self, src):
        i = self.wfm_i % 4
        self.wfm_i += 1
        v = self.wfm[i]
        key = ("wfm", i)
        self.dma("gpsimd", v, src, [], [key], key)
        return v, key

    def wtm_load(self, src, view=None):
        i = self.wtm_i % 2
        self.wtm_i += 1
        v = self.wtm[i] if view is None else view(i)
        key = ("wtm", i)
        self.dma("gpsimd", v, src, [], [key], key)
        return v, key

    def build(self):
        nc = bass.Bass("TRN2", target_bir_lowering=False)
        self.nc = nc
        din = lambda name, shape, dt=F32: nc.dram_tensor(name, shape, dt, kind="ExternalInput").ap()
        D = {}
        D["x"] = din("x", [2, S, 1024])
        D["cT"] = din("cT", [128, 8, 2])
        D["posT"] = din("posT", [128, 2, 16], I32)
        D["w_ada"] = din("w_ada", [48, 128, 8, 128])
        D["b_ada"] = din("b_ada", [128, 48])
        D["n1g"] = din("n1g", [128, 8])
        D["n2g"] = din("n2g", [128, 8])
        D["fing"] = din("fing", [1024])
        D["w_in_fm"] = din("w_in_fm", [32, 128, 8, 128])
        D["w_in_tm"] = din("w_in_tm", [3, 2, 128, 8, 512])
        D["crw"] = din("crw", [128, 8, 4])
        D["crb"] = din("crb", [128, 8])
        D["w_rg"] = din("w_rg", [2, 2, 16, 64, 64])
        D["brg"] = din("brg", [128, 2, 2, 8])
        D["lamrg"] = din("lamrg", [128, 2, 8])
        D["w_rnn_o"] = din("w_rnn_o", [8, 128, 8, 128])
        D["w_attn_o"] = din("w_attn_o", [8, 128, 8, 128])
        D["w_out"] = din("w_out", [2, 128, 8, 512])
        D["lamv"] = din("lamv", [4, 64])
        D["subg"] = din("subg", [128, 1])
        D["w_up"] = din("w_up", [44, 128, 8, 128])
        D["cfw"] = din("cfw", [128, 44, 3])
        D["cfb"] = din("cfb", [128, 44])
        D["w_down"] = din("w_down", [22, 128, 1024])
        self.D = D
        out_d = nc.dram_tensor("out", [2, S, 1024], F32, kind="ExternalOutput").ap()

        with ExitStack() as ctx:
            sb = lambda name, shape, dt: ctx.enter_context(nc.sbuf_tensor(name, shape, dt))
            ARENA = 187904
            self.arena = sb("arena", [128, ARENA // 4], F32)
            self.cst = sb("cst", [128, 1848], F32)
            self.coff = 0
            self.ident = sb("ident", [128, 128], BF16)
            cactb = sb("cactb", [128, 8, 2], BF16)
            rgw = sb("rgw", [128, 2, 2, 8, 128], BF16)
            gb = sb("gb", [128, 1024], F32)
            fingb = sb("fingb", [128, 1024], F32)
            psA = ctx.enter_context(nc.psum_tensor("psA", [128, 2048], F32))
            psB = ctx.enter_context(nc.psum_tensor("psB", [128, 2048], F32))
            P = Prog(nc, ctx)
            self.P = P

            def bank(i):
                t = psA if i < 4 else psB
                j = i % 4
                return t[:, j * 512:(j + 1) * 512]

            def bankbf(i):
                return bank(i).bitcast(BF16)

            pk = lambda i: ("ps", i)

            O_WFM, O_WTM, O_PT, O_SCR = 0, 8192, 24576, 27648
            O_A, O_B, O_C, O_D = 39936, 72704, 105472, 138240
            self.wfm = [self.view(O_WFM + i * 2048, 2048, BF16, "p (k c) -> p k c", c=128) for i in range(4)]
            self.wtm = [self.view(O_WTM + i * 8192, 8192, BF16, "p (k c) -> p k c", c=512) for i in range(2)]
            self.wfm_i = 0
            self.wtm_i = 0
            PT = [self.view(O_PT + i * 1024, 1024, BF16) for i in range(3)]

            def scr(i, dt, pattern=None, **kw):
                return self.view(O_SCR + i * 2048, 2048, dt, pattern, **kw)

            sk = lambda i: ("S", i)
            for nm, rg in [("hT", ["A"]), ("x", ["B", "C"]), ("OT", ["B"]), ("hgT", ["B"]), ("M", ["C"]),
                           ("Dtok", ["C"]), ("qT", ["D"]), ("kT", ["D"]), ("V", ["D"]), ("rgt", ["D"]),
                           ("x1", ["A", "B"]), ("h2T", ["C"]), ("ffn", ["D"])]:
                P.declare(nm, rg)

            hT = self.view(O_A, 32768, BF16, "p (k t) -> p k t", t=S)
            xres = self.view(O_B, 65536, F32, "p (n c) -> p n c", c=1024)
            OT = self.view(O_B, 32768, BF16, "p (k t) -> p k t", t=S)
            hgT = OT
            M = self.view(O_C, 32768, BF16, "p (k t) -> p k t", t=S)
            Dtok = self.view(O_C, 16384, BF16, "p (n c) -> p n c", c=512)
            qT = self.view(O_D, 16384, BF16, "p (h t) -> p h t", t=S)
            kT = self.view(O_D + 16384, 16384, BF16, "p (h t) -> p h t", t=S)
            V = self.view(O_D + 32768, 16640, BF16, "p (n h e) -> p n h e", h=4, e=130)
            x1 = self.view(O_A, 65536, F32, "p (n c) -> p n c", c=1024)
            h2T = self.view(O_C, 32768, BF16, "p (k t) -> p k t", t=S)

            cv = self.cview
            modT = cv(96, "p (k b) -> p k b", b=2)
            bada = cv(48)
            n1g = cv(8)
            n2g = cv(8)
            gs1 = cv(16, "p (k b) -> p k b", b=2)
            gs2 = cv(16, "p (k b) -> p k b", b=2)
            crw = cv(32, "p (k t) -> p k t", t=4)
            crb = cv(8)
            brg = cv(32, "p (g d k) -> p g d k", g=2, d=2)
            bh = cv(32, "p (g d k) -> p g d k", g=2, d=2)
            lamrg = cv(16, "p (d k) -> p d k", d=2)
            sp_ = cv(16, "p (d k) -> p d k", d=2)
            chalf = cv(16, "p (d k) -> p d k", d=2)
            cfull = cv(16, "p (d k) -> p d k", d=2)
            subg = cv(1)
            subgs = cv(1)
            lamq = cv(256, "p (v e) -> p v e", e=64)
            lprod = cv(128, "p (v e) -> p v e", e=64)
            ls = cv(2)
            le = cv(2)
            neglam = cv(1)
            cfw = cv(132, "p (j t) -> p j t", t=3)
            cfb = cv(44)
            ss = cv(16)
            rstd = cv(16)
            ssq = cv(64, "p (n h) -> p n h", h=4)
            rstda = cv(64, "p (n h) -> p n h", h=4)
            posi = cv(32).bitcast(I32).rearrange("p (b n) -> p b n", b=2)
            posf = cv(32, "p (b n) -> p b n", b=2)
            ang = cv(128, "p (n j) -> p n j", j=8)
            kf = cv(128, "p (n j) -> p n j", j=8)
            ki = cv(128).bitcast(I32).rearrange("p (n j) -> p n j", j=8)
            sinT = cv(128, "p (n j) -> p n j", j=8)
            cosT = cv(128, "p (n j) -> p n j", j=8)
            cact32 = cv(16, "p (k b) -> p k b", b=2)
            c1 = cv(1)
            c025 = cv(1)
            ceps = cv(1)
            rec = cv(8)
            gtmp = cv(8)
            ghi32 = cv(8)
            glo32 = cv(8)
            assert self.coff <= 1848, self.coff

            ident = self.ident
            P.add("gpsimd", lambda e: e.memset(ident[:], 0.0), w=["ident"])
            P.add("gpsimd", lambda e: e.affine_select(out=ident[:], in_=ident[:], compare_op=ALU.not_equal, fill=1.0,
                                                      base=0, pattern=[[-1, 128]], channel_multiplier=1),
                  r=["ident"], w=["ident"])
            P.add("gpsimd", lambda e: e.memset(c1, 1.0), w=["c1"])
            P.add("gpsimd", lambda e: e.memset(c025, 0.25), w=["c025"])
            P.add("gpsimd", lambda e: e.memset(ceps, EPS), w=["ceps"])
            rgwk = [("rgw", i) for i in range(8)]
            P.add("gpsimd", lambda e: e.memset(rgw[:], 0.0), w=rgwk)
            self.cload(cact32, D["cT"], "cact32")
            self.cload(posi, D["posT"], "posi")
            self.cload(bada, D["b_ada"], "bada")
            self.cload(n1g, D["n1g"], "n1g")
            self.cload(n2g, D["n2g"], "n2g")
            self.cload(fingb[:], D["fing"].partition_broadcast(128), "fingb")
            self.cload(crw, D["crw"], "crw")
            self.cload(crb, D["crb"], "crb")
            self.cload(brg, D["brg"], "brg")
            self.cload(lamrg, D["lamrg"], "lamrg")
            self.cload(subg, D["subg"], "subg")
            for v_ in range(4):
                self.cload(lamq[:, v_, :], D["lamv"][v_, :].partition_broadcast(128), ("lamq", v_))
            self.cload(cfw, D["cfw"], "cfw")
            self.cload(cfb, D["cfb"], "cfb")
            for g in range(2):
                for d in range(2):
                    for hb in range(2):
                        src = D["w_rg"][g, d].rearrange("(c h) k j -> h k c j", h=2)[hb]
                        dst = rgw[hb * 64:(hb + 1) * 64, g, d, :, hb * 64:(hb + 1) * 64]
                        self.dma("gpsimd", dst, src, [], [("rgw", g * 4 + d * 2 + hb)], "constg", waitall=True)
            self.tap("c_crw", crw, ["crw"])
            self.ck("c1")

            self.act(cactb[:], cact32, AF.Silu, ["cact32"], ["cactb"])
            self.tap("c_cactb", cactb[:], ["cactb"])
            self.tap("c_rgw", rgw[:, 0, 0, 0, :], rgwk)
            self.ck("c2")
            pend = deque()
            for ch in range(2):
                pend.append(self.wfm_load(D["w_ada"][ch]))
            for ch in range(48):
                wv, wk = pend.popleft()
                if ch + 2 < 48:
                    pend.append(self.wfm_load(D["w_ada"][ch + 2]))
                for kc in range(8):
                    self.mm(bank(0)[:, ch * 2:ch * 2 + 2], wv[:, kc, :], cactb[:, kc, :], kc == 0, kc == 7,
                            [wk, "cactb"], [pk(0)])
            self.tt("vector", modT, bank(0)[:, 0:96].rearrange("p (k b) -> p k b", b=2), bc_last(bada, 2), ALU.add,
                    [pk(0), "bada"], ["modT"])
            self.tap("c_modT", modT, ["modT"])
            self.ck("c3")
            for b in range(2):
                self.stt("vector", gs1[:, :, b], modT[:, 8:16, b], 1.0, n1g, ALU.add, ALU.mult, ["modT", "n1g"], [("gs1", b)])
                self.stt("vector", gs2[:, :, b], modT[:, 32:40, b], 1.0, n2g, ALU.add, ALU.mult, ["modT", "n2g"], [("gs2", b)])
            self.act(sp_, lamrg, AF.Exp, ["lamrg"], ["sp"], scale=-1.0)
            self.act(sp_, sp_, AF.Ln, ["sp", "c1"], ["sp"], bias=c1)
            self.ts("vector", chalf, sp_, -4.0, None, ALU.mult, None, ["sp"], ["chalf"])
            self.ts("vector", cfull, sp_, -8.0, None, ALU.mult, None, ["sp"], ["cfull"])
            self.ts("vector", bh, brg, 0.5, None, ALU.mult, None, ["brg"], ["bh"])
            self.tap("c_chalf", chalf, ["chalf"])
            self.ck("c4")
            for i in range(2):
                self.tt("vector", lprod[:, i, :], lamq[:, 2 * i, :], lamq[:, 2 * i + 1, :], ALU.mult,
                        [("lamq", 2 * i), ("lamq", 2 * i + 1)], [("lprod", i)])
                P.add("vector", lambda e, i=i: e.reduce_sum(ls[:, i:i + 1], lprod[:, i, :], axis=AX.X),
                      r=[("lprod", i)], w=[("ls", i)])
            self.act(le, ls, AF.Exp, [("ls", 0), ("ls", 1)], ["le"])
            self.tt("vector", neglam, le[:, 1:2], le[:, 0:1], ALU.subtract, ["le"], ["neglam"])
            self.ts("vector", neglam, neglam, -0.2, None, ALU.add, None, ["neglam"], ["neglam"])
            self.ts("vector", subgs, subg, 0.8, None, ALU.mult, None, ["subg"], ["subgs"])
            self.cp("vector", posf, posi, ["posi"], ["posf"])
            self.tap("modT", modT, ["modT"])

            invf = (np.float32(500000.0) ** (-np.arange(0, 16, 2, dtype=np.float32) / np.float32(16))).astype(np.float32)

            def range_reduce_sin(dst):
                self.ts("vector", kf, ang, 1.0 / (2 * PI), None, ALU.mult, None, ["ang"], ["kf"])
                self.cp("vector", ki, kf, ["kf"], ["ki"])
                self.cp("vector", kf, ki, ["ki"], ["kf"])
                self.stt("vector", ang, kf, -2 * PI, ang, ALU.mult, ALU.add, ["kf", "ang"], ["ang"])
                self.ts("vector", kf, ang, PI, -2 * PI, ALU.is_gt, ALU.mult, ["ang"], ["kf"])
                self.tt("vector", ang, ang, kf, ALU.add, ["ang", "kf"], ["ang"])
                self.ts("vector", kf, ang, -PI, 2 * PI, ALU.is_lt, ALU.mult, ["ang"], ["kf"])
                self.tt("vector", ang, ang, kf, ALU.add, ["ang", "kf"], ["ang"])
                self.act(dst, ang, AF.Sin, ["ang"], [("rope", 0)])

            def gate_row(b, chunk0, scale):
                gcol = modT[:, chunk0:chunk0 + 8, b]
                self.ts("vector", gtmp, gcol, scale, None, ALU.mult, None, ["modT"], ["gtmp"])
                GH = scr(0, BF16, "p (k c) -> p k c", c=128)
                GL = scr(1, BF16, "p (k c) -> p k c", c=128)
                self.cp("vector", GH, bc_last(gtmp, 128), ["gtmp"], [sk(0)])
                self.cp("vector", ghi32, GH[:, :, 0], [sk(0)], ["ghi32"])
                self.tt("vector", glo32, gtmp, ghi32, ALU.subtract, ["gtmp", "ghi32"], ["glo32"])
                self.cp("vector", GL, bc_last(glo32, 128), ["glo32"], [sk(1)])
                for kc in range(8):
                    bk = kc // 4
                    o = bank(bk)[:, (kc % 4) * 128:(kc % 4 + 1) * 128]
                    self.mm(o, GH[:, kc, :], ident[:], True, False, [sk(0), "ident"], [pk(bk)])
                    self.mm(o, GL[:, kc, :], ident[:], False, True, [sk(1), "ident"], [pk(bk)])
                for bk in range(2):
                    self.cp("vector", gb[:, bk * 512:(bk + 1) * 512], bank(bk), [pk(bk)], ["gb"])

            try:
                self.body(locals())
            except StopBuild:
                pass
            P.emit()
        return nc

    def body(self, L):
        globals_ = L
        (P, D, hT, xres, OT, hgT, M, Dtok, qT, kT, V, x1, h2T, modT, gs1, gs2, crw, crb, bh, chalf, cfull, subgs, neglam,
         cfw, cfb, ss, rstd, ssq, rstda, posf, ang, sinT, cosT, c025, ceps, rec, rgw, gb, fingb, psA, psB, bank, bankbf, pk, sk,
         scr, PT, invf, range_reduce_sin, gate_row, out_d, O_SCR, O_D, O_WTM) = [L[k] for k in (
            "P", "D", "hT", "xres", "OT", "hgT", "M", "Dtok", "qT", "kT", "V", "x1", "h2T", "modT", "gs1", "gs2", "crw", "crb",
            "bh", "chalf", "cfull", "subgs", "neglam", "cfw", "cfb", "ss", "rstd", "ssq", "rstda", "posf", "ang", "sinT", "cosT",
            "c025", "ceps", "rec", "rgw", "gb", "fingb", "psA", "psB", "bank", "bankbf", "pk", "sk", "scr", "PT", "invf",
            "range_reduce_sin", "gate_row", "out_d", "O_SCR", "O_D", "O_WTM")]
        if True:
            self.ck("p0")
            for b in range(2):
                for j in range(8):
                    self.ts("vector", ang[:, :, j], posf[:, b, :], float(invf[j]), None, ALU.mult, None, ["posf"], ["ang"])
                range_reduce_sin(sinT)
                for j in range(8):
                    self.ts("vector", ang[:, :, j], posf[:, b, :], float(invf[j]), None, ALU.mult, None, ["posf"], ["ang"])
                self.ts("vector", ang, ang, PI / 2, None, ALU.add, None, ["ang"], ["ang"])
                range_reduce_sin(cosT)

                for rg in ("A", "B", "C"):
                    P.handoff(rg)
                sskeys = [("ss", tt) for tt in range(NT)]
                P.add("vector", lambda e: e.memset(ss, 0.0), w=sskeys)
                for tt in range(NT):
                    self.dma("sync", xres[:, tt, :], D["x"][b, tt * 128:(tt + 1) * 128, :], [], [("x", tt)], ("x", tt))
                    self.act(scr(0, BF16), xres[:, tt, :], AF.Square, [("x", tt)], [sk(0), ("ss", tt)], accum=ss[:, tt:tt + 1])
                sskeys = [("ss", tt) for tt in range(NT)]
                self.ts("vector", rstd, ss, 1.0 / 1024, None, ALU.mult, None, sskeys, ["rstd"])
                self.act(rstd, rstd, AF.Sqrt, ["rstd", "ceps"], ["rstd"], bias=ceps)
                P.add("vector", lambda e: e.reciprocal(rstd, rstd), r=["rstd"], w=["rstd"])
                for tt in range(NT):
                    si = 1 + tt % 2
                    xnb = scr(si, BF16)
                    self.ts("vector", xnb, xres[:, tt, :], rstd[:, tt:tt + 1], None, ALU.mult, None,
                            [("x", tt), "rstd"], [sk(si)])
                    bk = tt % 4
                    pv = bankbf(bk).rearrange("p (k c) -> p k c", c=128)
                    for kc in range(8):
                        self.tr(pv[:, kc, :], xnb[:, kc * 128:(kc + 1) * 128], [sk(si)], [pk(bk)])
                    for kc in range(8):
                        o = hT[:, kc, tt * 128:(tt + 1) * 128]
                        if tt % 2 == 0:
                            self.act(o, pv[:, kc, :], AF.Identity, [pk(bk), ("gs1", b), "modT"], [("hT", tt)],
                                     bias=modT[:, kc, b:b + 1], scale=gs1[:, kc, b:b + 1])
                        else:
                            self.ts("vector", o, pv[:, kc, :], gs1[:, kc, b:b + 1], modT[:, kc, b:b + 1], ALU.mult, ALU.add,
                                    [pk(bk), ("gs1", b), "modT"], [("hT", tt)])
                if b == 0:
                    self.tap("hT", hT, [("hT", tt) for tt in range(NT)])
                    self.ck("p1")
                hTk = lambda tb: [("hT", 4 * tb + i) for i in range(4)]

                for rg in ("B", "C", "D"):
                    P.handoff(rg)
                for hg in range(2):
                    P.add("gpsimd", lambda e: e.memset(V[:, :, :, 128:130], 1.0), w=[("V", tt) for tt in range(NT)])
                    pend = deque()
                    pend.append(self.wtm_load(D["w_in_tm"][0, hg]))
                    for wi in range(3):
                        wv, wk = pend.popleft()
                        if wi + 1 < 3:
                            pend.append(self.wtm_load(D["w_in_tm"][wi + 1, hg]))
                        for tt in range(NT):
                            bk = tt % 2
                            for kc in range(8):
                                self.mm(bank(bk), hT[:, kc, tt * 128:(tt + 1) * 128], wv[:, kc, :], kc == 0, kc == 7,
                                        [("hT", tt), wk], [pk(bk)])
                            if wi == 2:
                                self.cp("scalar", V[:, tt, :, 0:128], bank(bk).rearrange("p (h e) -> p h e", e=128),
                                        [pk(bk)], [("V", tt)])
                                continue
                            si = 3 + tt % 2
                            fi = tt % 2
                            qtok = scr(si, BF16)[:, 0:512]
                            q3 = qtok.rearrange("p (g e) -> p g e", e=64)
                            qf = scr(fi, F32)
                            p3 = qf.rearrange("p (g e) -> p g e", e=64)
                            rt = self.view(O_SCR + 5 * 2048, 1024, F32, "p (a g e) -> p a g e", a=4, g=8)
                            self.cp("scalar", qf, bank(bk), [pk(bk)], [sk(fi)])
                            self.cp("vector", qtok, qf, [sk(fi)], [sk(si)])
                            cs = bc_mid(cosT[:, tt, :], 8)
                            sn = bc_mid(sinT[:, tt, :], 8)
                            rk = [sk(fi), ("rope", 0)]
                            self.tt("vector", rt[:, 0], p3[:, :, 0:8], cs, ALU.mult, rk, [("rt", 0)])
                            self.tt("vector", rt[:, 1], p3[:, :, 8:16], sn, ALU.mult, rk, [("rt", 1)])
                            self.tt("vector", q3[:, :, 0:8], rt[:, 0], rt[:, 1], ALU.subtract, [("rt", 0), ("rt", 1)], [sk(si)])
                            self.tt("vector", rt[:, 2], p3[:, :, 8:16], cs, ALU.mult, rk, [("rt", 2)])
                            self.tt("vector", rt[:, 3], p3[:, :, 0:8], sn, ALU.mult, rk, [("rt", 3)])
                            self.tt("vector", q3[:, :, 8:16], rt[:, 2], rt[:, 3], ALU.add, [("rt", 2), ("rt", 3)], [sk(si)])
                            tb_ = 2 + tt % 2
                            pv = bankbf(tb_)[:, 0:512].rearrange("p (h c) -> p h c", c=128)
                            for hl in range(4):
                                self.tr(pv[:, hl, :], qtok[:, hl * 128:(hl + 1) * 128], [sk(si)], [pk(tb_)])
                            dstT = qT if wi == 0 else kT
                            nm = "qT" if wi == 0 else "kT"
                            self.cp("vector", dstT[:, :, tt * 128:(tt + 1) * 128], pv, [pk(tb_)], [(nm, tt)])
                            if b == 0 and hg == 0 and wi == 0:
                                self.ck("p2t%d" % tt)
                            if b == 0 and hg == 0 and tt == NT - 1:
                                self.tap("w%d" % wi, dstT, [(nm, t_) for t_ in range(NT)])
                                self.ck("p2w%d" % wi)
                            if b == 0 and hg == 0 and wi == 0 and tt == 0:
                                self.tap("qtok0", qtok, [sk(si)])
                                self.tap("qT0", qT[:, :, 0:128], [("qT", 0)])
                                self.ck("p2a0")
                    if b == 0 and hg == 0:
                        self.tap("qT", qT, [("qT", tt) for tt in range(NT)])
                        self.tap("kT", kT, [("kT", tt) for tt in range(NT)])
                        self.tap("V", V, [("V", tt) for tt in range(NT)])
                        self.ck("p2a")
                    sti = 0
                    for hl in range(4):
                        for qb in range(4):
                            qk_ = [("qT", 4 * qb + i) for i in range(4)]
                            for c in range(2):
                                oacc = psB[:, c * 1024:(c + 1) * 1024].rearrange("p (s w) -> p s w", w=256)
                                ok = [pk(4 + 2 * c), pk(5 + 2 * c)]
                                for kt in range(NT):
                                    sb_ = sti % 3
                                    sti += 1
                                    self.mm(bank(sb_), kT[c * 64:(c + 1) * 64, hl, kt * 128:(kt + 1) * 128],
                                            qT[c * 64:(c + 1) * 64, hl, qb * 512:(qb + 1) * 512], True, True,
                                            [("kT", kt)] + qk_, [pk(sb_)])
                                    self.act(PT[sb_], bank(sb_), AF.Exp, [pk(sb_)], [("PT", sb_)], scale=0.125)
                                    for s_ in range(4):
                                        self.mm(oacc[:, s_, 0:129], PT[sb_][:, s_ * 128:(s_ + 1) * 128], V[:, kt, hl, 0:129],
                                                kt == 0 and s_ % 2 == 0, kt == NT - 1 and s_ % 2 == 1,
                                                [("PT", sb_), ("V", kt)], [ok[s_ // 2]], skip=True)
                                rc = rec[:, c * 4:(c + 1) * 4]
                                P.add("vector", lambda e, rc=rc, oacc=oacc: e.reciprocal(rc, oacc[:, :, 128]), r=ok, w=[("rec", c)])
                                if c == 0:
                                    A0 = scr(0, F32, "p (s e) -> p s e", e=128)
                                    self.tt("vector", A0, oacc[:, :, 0:128], bc_last(rc, 128), ALU.mult, ok + [("rec", 0)], [sk(0)])
                                else:
                                    T_ = scr(1, F32, "p (s e) -> p s e", e=128)
                                    SQ = scr(2, F32, "p (s e) -> p s e", e=128)
                                    self.ts("vector", rc, rc, neglam, None, ALU.mult, None, [("rec", 1), "neglam"], [("rec", 1)])
                                    self.tt("vector", T_, oacc[:, :, 0:128], bc_last(rc, 128), ALU.mult, ok + [("rec", 1)], [sk(1)])
                                    self.tt("vector", T_, T_, A0, ALU.add, [sk(0), sk(1)], [sk(1)])
                                    dk = [("Dtok", 4 * qb + i) for i in range(4)]
                                    self.cp("gpsimd", Dtok[:, 4 * qb:4 * qb + 4, hl * 128:(hl + 1) * 128], T_, [sk(1)], dk)
                                    self.tt("vector", SQ, T_, T_, ALU.mult, [sk(1)], [sk(2)])
                                    P.add("vector", lambda e, SQ=SQ, qb=qb, hl=hl: e.reduce_sum(ssq[:, 4 * qb:4 * qb + 4, hl], SQ, axis=AX.X),
                                          r=[sk(2)], w=[("ssq", qb, hl)])
                                    if b == 0 and hg == 0 and hl == 0 and qb == 0:
                                        self.tap("Dtok0", Dtok[:, 0:4, 0:128], dk)
                                        self.tap("ssq0", ssq[:, 0:4, 0], [("ssq", 0, 0)])
                                        self.ck("p2b")
                    sqk = [("ssq", qb, hl) for qb in range(4) for hl in range(4)]
                    self.ts("vector", rstda, ssq, 1.0 / 128, None, ALU.mult, None, sqk, ["rstda"])
                    self.act(rstda, rstda, AF.Sqrt, ["rstda", "ceps"], ["rstda"], bias=ceps)
                    P.add("vector", lambda e: e.reciprocal(rstda, rstda), r=["rstda"], w=["rstda"])
                    for tt in range(NT):
                        si = 3 + tt % 2
                        Dn = scr(si, BF16)[:, 0:512]
                        self.tt("vector", Dn.rearrange("p (h e) -> p h e", e=128),
                                Dtok[:, tt, :].rearrange("p (h e) -> p h e", e=128), bc_last(rstda[:, tt, :], 128), ALU.mult,
                                [("Dtok", tt), "rstda"], [sk(si)])
                        bk = tt % 2
                        pv = bankbf(bk)[:, 0:512].rearrange("p (h c) -> p h c", c=128)
                        for hl in range(4):
                            self.tr(pv[:, hl, :], Dn[:, hl * 128:(hl + 1) * 128], [sk(si)], [pk(bk)])
                        self.ts("vector", OT[:, hg * 4:(hg + 1) * 4, tt * 128:(tt + 1) * 128], pv, subgs, None, ALU.mult, None,
                                [pk(bk), "subgs"], [("OT", tt)])
                if b == 0:
                    self.tap("OT", OT, [("OT", tt) for tt in range(NT)])
                    self.ck("p2")
                OTk = lambda tb: [("OT", 4 * tb + i) for i in range(4)]

                def merge(gate_ch0, wsrc, srcT, srck, first):
                    pend = deque()
                    pend.append((self.wfm_load(D["w_in_fm"][gate_ch0]), self.wfm_load(wsrc[0])))
                    for j in range(8):
                        (wg, wgk), (wo, wok) = pend.popleft()
                        if j + 1 < 8:
                            pend.append((self.wfm_load(D["w_in_fm"][gate_ch0 + j + 1]), self.wfm_load(wsrc[j + 1])))
                        for tb in range(4):
                            bx, by = 2 * (tb % 2), 2 * (tb % 2) + 1
                            cols = slice(tb * 512, (tb + 1) * 512)
                            for kc in range(8):
                                self.mm(bank(bx), wg[:, kc, :], hT[:, kc, cols], kc == 0, kc == 7, [wgk] + hTk(tb), [pk(bx)])
                            for kc in range(8):
                                self.mm(bank(by), wo[:, kc, :], srcT[:, kc, cols], kc == 0, kc == 7, [wok] + srck(tb), [pk(by)])
                            si = tb % 2
                            tg = scr(si, F32)
                            self.act(tg, bank(bx), AF.Tanh, [pk(bx)], [sk(si)], scale=0.5)
                            mk = [("M", j, tb)]
                            if first:
                                self.stt("vector", M[:, j, cols], tg, 1.0, bank(by), ALU.add, ALU.mult, [sk(si), pk(by)], mk)
                            else:
                                mt = scr(2 + si, F32)
                                self.stt("vector", mt, tg, 1.0, bank(by), ALU.add, ALU.mult, [sk(si), pk(by)], [sk(2 + si)])
                                self.tt("gpsimd", M[:, j, cols], mt, M[:, j, cols], ALU.add, [sk(2 + si)] + mk, mk)

                P.handoff("C")
                merge(24, D["w_attn_o"], OT, OTk, True)
                if b == 0:
                    self.ck("p3")

                P.handoff("B")
                P.handoff("D")
                xc32 = self.view(O_D, 8192, F32)
                xcb = self.view(O_D + 8192, 4096, BF16)
                G = self.view(O_D + 12288, 4096, BF16)
                T2 = self.view(O_D + 16384, 8192, F32)
                T3 = self.view(O_D + 24576, 8192, F32)
                Aa = [self.view(O_D + 32768, 8192, F32), self.view(O_D + 40960, 8192, F32)]
                pAk = [pk(i) for i in range(4)]
                pBk = [pk(4 + i) for i in range(4)]
                allh = [("hT", tt) for tt in range(NT)]
                pend = deque()
                pend.append((self.wfm_load(D["w_in_fm"][0]), self.wfm_load(D["w_in_fm"][8])))
                for c in range(8):
                    (wx, wxk), (wy, wyk) = pend.popleft()
                    if c + 1 < 8:
                        pend.append((self.wfm_load(D["w_in_fm"][c + 1]), self.wfm_load(D["w_in_fm"][8 + c + 1])))
                    for kc in range(8):
                        for tb in range(4):
                            self.mm(bank(tb), wx[:, kc, :], hT[:, kc, tb * 512:(tb + 1) * 512], kc == 0, kc == 7,
                                    [wxk] + hTk(tb), [pk(tb)])
                    for kc in range(8):
                        for tb in range(4):
                            self.mm(bank(4 + tb), wy[:, kc, :], hT[:, kc, tb * 512:(tb + 1) * 512], kc == 0, kc == 7,
                                    [wyk] + hTk(tb), [pk(4 + tb)])
                    self.act(xc32, psA[:, :], AF.Identity, pAk + ["crw", "crb"], [("rgt", "xc")], bias=crb[:, c:c + 1], scale=crw[:, c, 2:3])
                    self.stt("vector", xc32[:, 2:S], psA[:, 0:S - 2], crw[:, c, 0:1], xc32[:, 2:S], ALU.mult, ALU.add,
                             pAk + [("rgt", "xc"), "crw"], [("rgt", "xc")])
                    self.stt("vector", xc32[:, 1:S], psA[:, 0:S - 1], crw[:, c, 1:2], xc32[:, 1:S], ALU.mult, ALU.add,
                             pAk + [("rgt", "xc"), "crw"], [("rgt", "xc")])
                    self.stt("vector", xc32[:, 0:S - 1], psA[:, 1:S], crw[:, c, 3:4], xc32[:, 0:S - 1], ALU.mult, ALU.add,
                             pAk + [("rgt", "xc"), "crw"], [("rgt", "xc")])
                    self.cp("gpsimd", xcb, xc32, [("rgt", "xc")], [("rgt", "xcb")])
                    self.act(T2, psB[:, :], AF.Square, pBk, [("rgt", "T2")])
                    self.ts("vector", T2, T2, 0.044715, 1.0, ALU.mult, ALU.add, [("rgt", "T2")], [("rgt", "T2")])
                    self.tt("vector", T2, psB[:, :], T2, ALU.mult, pBk + [("rgt", "T2")], [("rgt", "T2")])
                    self.act(T2, T2, AF.Tanh, [("rgt", "T2")], [("rgt", "T2")], scale=0.7978845608028654)
                    self.stt("vector", G, T2, 1.0, psB[:, :], ALU.add, ALU.mult, pBk + [("rgt", "T2")], [("rgt", "G")])
                    for d in range(2):
                        A_ = Aa[d]
                        ak = ("rgt", "A", d)
                        for tb in range(4):
                            self.mm(bank(tb), rgw[:, 0, d, c, :], xcb[:, tb * 512:(tb + 1) * 512], True, True,
                                    L["rgwk"] + [("rgt", "xcb")], [pk(tb)])
                        for tb in range(4):
                            self.mm(bank(4 + tb), rgw[:, 1, d, c, :], xcb[:, tb * 512:(tb + 1) * 512], True, True,
                                    L["rgwk"] + [("rgt", "xcb")], [pk(4 + tb)])
                        self.act(T2, psA[:, :], AF.Tanh, pAk + ["bh"], [("rgt", "T2")], bias=bh[:, 0, d, c:c + 1], scale=0.5)
                        self.act(T3, psB[:, :], AF.Tanh, pBk + ["bh"], [("rgt", "T3")], bias=bh[:, 1, d, c:c + 1], scale=0.5)
                        self.act(A_, T2, AF.Exp, [("rgt", "T2"), "chalf"], [ak], bias=chalf[:, d, c:c + 1], scale=chalf[:, d, c:c + 1])
                        self.act(T2, T2, AF.Exp, [("rgt", "T2"), "cfull"], [("rgt", "T2")], bias=cfull[:, d, c:c + 1], scale=cfull[:, d, c:c + 1])
                        self.act(T2, T2, AF.Sqrt, [("rgt", "T2"), "c025"], [("rgt", "T2")], bias=c025, scale=-0.25)
                        self.stt("vector", T3, T3, 1.0, xc32, ALU.add, ALU.mult, [("rgt", "T3"), ("rgt", "xc")], [("rgt", "T3")])
                        self.tt("vector", T3, T3, T2, ALU.mult, [("rgt", "T3"), ("rgt", "T2")], [("rgt", "T3")])
                        if d == 0:
                            P.add("vector", lambda e, A_=A_: e.tensor_tensor_scan(A_, A_, T3, 0.0, ALU.mult, ALU.add),
                                  r=[ak, ("rgt", "T3")], w=[ak])
                        else:
                            P.add("vector", lambda e, A_=A_: e.tensor_tensor_scan(A_[:, ::-1], A_[:, ::-1], T3[:, ::-1], 0.0, ALU.mult, ALU.add),
                                  r=[ak, ("rgt", "T3")], w=[ak])
                    self.tt("gpsimd", Aa[0], Aa[0], Aa[1], ALU.add, [("rgt", "A", 0), ("rgt", "A", 1)], [("rgt", "A", 0)])
                    self.stt("vector", hgT[:, c, :], Aa[0], 0.5, G, ALU.mult, ALU.mult, [("rgt", "A", 0), ("rgt", "G")],
                             [("hgT", c)])
                if b == 0:
                    self.tap("hgT", hgT, [("hgT", c) for c in range(8)])
                    self.ck("p4")
                hgk = lambda tb: [("hgT", c) for c in range(8)]

                merge(16, D["w_rnn_o"], hgT, hgk, False)
                if b == 0:
                    self.tap("M", M, [("M", j, tb) for j in range(8) for tb in range(4)])
                    self.ck("p5")

                gate_row(b, 16, 0.5)
                P.handoff("A")
                P.handoff("B")
                P.add("vector", lambda e: e.memset(ss, 0.0), w=sskeys)
                wo_ = []
                for g in range(2):
                    wo_.append(self.wtm_load(D["w_out"][g]))
                for tt in range(NT):
                    self.dma("sync", x1[:, tt, :], D["x"][b, tt * 128:(tt + 1) * 128, :], [], [("x1", tt)], ("x", tt))
                for tt in range(NT):
                    mkeys = [("M", j, tt // 4) for j in range(8)]
                    for g in range(2):
                        bk = (2 * tt + g) % 4
                        wv, wk = wo_[g]
                        for kc in range(8):
                            self.mm(bank(bk), M[:, kc, tt * 128:(tt + 1) * 128], wv[:, kc, :], kc == 0, kc == 7, mkeys + [wk], [pk(bk)])
                        si = (2 * tt + g) % 2
                        tmp = scr(si, F32)
                        cols = slice(g * 512, (g + 1) * 512)
                        self.tt("vector", tmp, bank(bk), gb[:, cols], ALU.mult, [pk(bk), "gb"], [sk(si)])
                        self.tt("gpsimd", x1[:, tt, cols], tmp, x1[:, tt, cols], ALU.add, [sk(si), ("x1", tt)], [("x1", tt)])
                    self.act(scr(2, BF16), x1[:, tt, :], AF.Square, [("x1", tt)], [sk(2), ("ss", tt)], accum=ss[:, tt:tt + 1])
                if b == 0:
                    self.tap("x1", x1, [("x1", tt) for tt in range(NT)])
                    self.ck("p6")

                P.handoff("C")
                P.handoff("D")
                self.ts("vector", rstd, ss, 1.0 / 1024, None, ALU.mult, None, sskeys, ["rstd"])
                self.act(rstd, rstd, AF.Sqrt, ["rstd", "ceps"], ["rstd"], bias=ceps)
                P.add("vector", lambda e: e.reciprocal(rstd, rstd), r=["rstd"], w=["rstd"])
                for tt in range(NT):
                    si = 3 + tt % 2
                    xnb = scr(si, BF16)
                    self.ts("vector", xnb, x1[:, tt, :], rstd[:, tt:tt + 1], None, ALU.mult, None, [("x1", tt), "rstd"], [sk(si)])
                    bk = tt % 4
                    pv = bankbf(bk).rearrange("p (k c) -> p k c", c=128)
                    for kc in range(8):
                        self.tr(pv[:, kc, :], xnb[:, kc * 128:(kc + 1) * 128], [sk(si)], [pk(bk)])
                    for kc in range(8):
                        o = h2T[:, kc, tt * 128:(tt + 1) * 128]
                        if tt % 2 == 0:
                            self.act(o, pv[:, kc, :], AF.Identity, [pk(bk), ("gs2", b), "modT"], [("h2T", tt)],
                                     bias=modT[:, 24 + kc, b:b + 1], scale=gs2[:, kc, b:b + 1])
                        else:
                            self.ts("vector", o, pv[:, kc, :], gs2[:, kc, b:b + 1], modT[:, 24 + kc, b:b + 1], ALU.mult, ALU.add,
                                    [pk(bk), ("gs2", b), "modT"], [("h2T", tt)])
                gate_row(b, 40, 1.0)
                h2k = lambda tb: [("h2T", 4 * tb + i) for i in range(4)]
                aT = self.view(O_D, 24576, BF16, "p (j t) -> p j t", t=S)
                Yv = self.view(O_D + 24576, 8192, F32)
                Yg = self.view(O_D + 32768, 8192, F32)
                wdn = self.view(O_WTM, 16384, BF16, "p (k c) -> p k c", c=1024)
                groups = [(0, 6), (6, 6), (12, 5), (17, 5)]
                for (j0, nj) in groups:
                    self.dma("gpsimd", wdn[:, 0:nj, :], D["w_down"][j0:j0 + nj].rearrange("k p c -> p k c"), [],
                             [("wtm", 0), ("wtm", 1)], ("wtm", 0))
                    pend = deque()
                    pend.append((self.wfm_load(D["w_up"][j0]), self.wfm_load(D["w_up"][22 + j0])))
                    for jl in range(nj):
                        j = j0 + jl
                        (wv_, wvk), (wg_, wgk) = pend.popleft()
                        if jl + 1 < nj:
                            pend.append((self.wfm_load(D["w_up"][j + 1]), self.wfm_load(D["w_up"][22 + j + 1])))
                        for kc in range(8):
                            for tb in range(4):
                                self.mm(bank(tb), wv_[:, kc, :], h2T[:, kc, tb * 512:(tb + 1) * 512], kc == 0, kc == 7,
                                        [wvk] + h2k(tb), [pk(tb)])
                        for kc in range(8):
                            for tb in range(4):
                                self.mm(bank(4 + tb), wg_[:, kc, :], h2T[:, kc, tb * 512:(tb + 1) * 512], kc == 0, kc == 7,
                                        [wgk] + h2k(tb), [pk(4 + tb)])
                        for (Y, ps_, pks, jj, yk) in ((Yv, psA, pAk, j, ("ffn", "Yv")), (Yg, psB, pBk, 22 + j, ("ffn", "Yg"))):
                            self.act(Y, ps_[:, :], AF.Identity, pks + ["cfw", "cfb"], [yk], bias=cfb[:, jj:jj + 1], scale=cfw[:, jj, 1:2])
                            self.stt("vector", Y[:, 1:S], ps_[:, 0:S - 1], cfw[:, jj, 0:1], Y[:, 1:S], ALU.mult, ALU.add,
                                     pks + [yk, "cfw"], [yk])
                            self.stt("vector", Y[:, 0:S - 1], ps_[:, 1:S], cfw[:, jj, 2:3], Y[:, 0:S - 1], ALU.mult, ALU.add,
                                     pks + [yk, "cfw"], [yk])
                        self.act(Yg, Yg, AF.Silu, [("ffn", "Yg")], [("ffn", "Yg")])
                        self.tt("vector", aT[:, jl, :], Yg, Yv, ALU.mult, [("ffn", "Yg"), ("ffn", "Yv")], [("ffn", "aT", jl)])
                    if b == 0 and j0 == 0:
                        self.tap("aT", aT, [("ffn", "aT", jl) for jl in range(6)])
                        self.ck("p7")
                    ak_ = [("ffn", "aT", jl) for jl in range(nj)]
                    for tt in range(NT):
                        for g in range(2):
                            bk = (2 * tt + g) % 4
                            cols = slice(g * 512, (g + 1) * 512)
                            for jl in range(nj):
                                self.mm(bank(bk), aT[:, jl, tt * 128:(tt + 1) * 128], wdn[:, jl, cols], jl == 0, jl == nj - 1,
                                        ak_ + [("wtm", 0), ("wtm", 1)], [pk(bk)])
                            si = (2 * tt + g) % 2
                            tmp = scr(si, F32)
                            self.tt("vector", tmp, bank(bk), gb[:, cols], ALU.mult, [pk(bk), "gb"], [sk(si)])
                            self.tt("gpsimd", x1[:, tt, cols], tmp, x1[:, tt, cols], ALU.add, [sk(si), ("x1", tt)], [("x1", tt)])
                P.add("vector", lambda e: e.memset(ss, 0.0), w=sskeys)
                for tt in range(NT):
                    self.act(scr(2, BF16), x1[:, tt, :], AF.Square, [("x1", tt)], [sk(2), ("ss", tt)], accum=ss[:, tt:tt + 1])
                self.ts("vector", rstd, ss, 1.0 / 1024, None, ALU.mult, None, sskeys, ["rstd"])
                self.act(rstd, rstd, AF.Sqrt, ["rstd", "ceps"], ["rstd"], bias=ceps)
                P.add("vector", lambda e: e.reciprocal(rstd, rstd), r=["rstd"], w=["rstd"])
                for tt in range(NT):
                    self.stt("vector", x1[:, tt, :], x1[:, tt, :], rstd[:, tt:tt + 1], fingb[:], ALU.mult, ALU.mult,
                             [("x1", tt), "rstd", "fingb"], [("x1", tt)])
                    self.dma("sync", out_d[b, tt * 128:(tt + 1) * 128, :], x1[:, tt, :], [("x1", tt)], [], ("x", tt), outflag=True)


def fm_layout(w, nch):
    return np.ascontiguousarray(w.reshape(8, 128, nch, 128).transpose(2, 1, 0, 3))


def tm_layout(w, ng):
    return np.ascontiguousarray(w.reshape(8, 128, ng, 512).transpose(2, 1, 0, 3))


def pcol(v, n):
    return np.ascontiguousarray(v.reshape(n, 128).T)


_NC_CACHE = {}


def prepare_shared(inp):
    sh = {}
    sh["w_ada"] = fm_layout(inp["w_ada"][0], 48)
    sh["b_ada"] = pcol(inp["b_ada"][0], 48)
    sh["n1g"] = pcol(inp["norm1_g"][0], 8)
    sh["n2g"] = pcol(inp["norm2_g"][0], 8)
    sh["fing"] = np.ascontiguousarray(inp["final_g"])
    w_in = inp["w_in"][0]
    sh["w_in_fm"] = np.concatenate([fm_layout(w_in[:, 0:2048], 16), fm_layout(w_in[:, 5120:7168], 16)], axis=0)
    sh["w_in_tm"] = np.stack([tm_layout(w_in[:, 2048:3072], 2), tm_layout(w_in[:, 3072:4096], 2),
                              tm_layout(w_in[:, 4096:5120], 2)], axis=0)
    sh["crw"] = np.ascontiguousarray(inp["conv_rnn_w"][0].reshape(4, 8, 128).transpose(2, 1, 0))
    sh["crb"] = pcol(inp["conv_rnn_b"][0], 8)
    sh["w_rg"] = np.ascontiguousarray(np.stack([inp["w_rg_a"][0], inp["w_rg_i"][0]], axis=0))
    brg = np.stack([inp["b_rg_a"][0], inp["b_rg_i"][0]], axis=0)
    sh["brg"] = np.ascontiguousarray(brg.reshape(2, 2, 8, 128).transpose(3, 0, 1, 2))
    sh["lamrg"] = np.ascontiguousarray(inp["rg_lambda"][0].reshape(2, 8, 128).transpose(2, 0, 1))
    sh["w_rnn_o"] = fm_layout(inp["w_rnn_o"][0], 8)
    sh["w_attn_o"] = fm_layout(inp["w_attn_o"][0], 8)
    sh["w_out"] = tm_layout(inp["w_out"][0], 2)
    sh["lamv"] = np.ascontiguousarray(np.stack([inp["lam_q1"][0], inp["lam_k1"][0], inp["lam_q2"][0], inp["lam_k2"][0]], axis=0))
    sh["subg"] = np.ascontiguousarray(inp["subln_g"][0].reshape(128, 1))
    sh["w_up"] = fm_layout(inp["w_up"][0], 44)
    sh["cfw"] = np.ascontiguousarray(inp["conv_ffn_w"][0].reshape(3, 44, 128).transpose(2, 1, 0))
    sh["cfb"] = pcol(inp["conv_ffn_b"][0], 44)
    sh["w_down"] = np.ascontiguousarray(inp["w_down"][0].reshape(22, 128, 1024))
    return {k: np.ascontiguousarray(v, dtype=np.float32) for k, v in sh.items()}


def make_in_maps(inp, n_cores=8):
    sh = prepare_shared(inp)
    maps = []
    for i in range(n_cores):
        m = dict(sh)
        m["x"] = np.ascontiguousarray(inp["x"][2 * i:2 * i + 2], dtype=np.float32)
        c2 = inp["c"][2 * i:2 * i + 2]
        m["cT"] = np.ascontiguousarray(c2.reshape(2, 8, 128).transpose(2, 1, 0), dtype=np.float32)
        p2 = inp["positions"][2 * i:2 * i + 2]
        m["posT"] = np.ascontiguousarray(p2.reshape(2, 16, 128).transpose(2, 0, 1), dtype=np.int32)
        maps.append(m)
    return maps


def kernel(**inputs):
    inp = {k: np.asarray(v) for k, v in inputs.items()}
    if "nc" not in _NC_CACHE:
        _NC_CACHE["nc"] = Builder().build()
    nc = _NC_CACHE["nc"]
    maps = make_in_maps(inp, 8)
    res = run_bass_kernel_spmd(nc, maps, core_ids=list(range(8)))
    out = np.concatenate([r["out"] for r in res.results], axis=0)
    return out.astype(np.float32)
```

```python
import numpy as np
from collections import deque
from contextlib import ExitStack
import concourse.bass as bass
import concourse.mybir as mybir
from concourse.bass_utils import run_bass_kernel_spmd

F32 = mybir.dt.float32
BF16 = mybir.dt.bfloat16
I32 = mybir.dt.int32
AF = mybir.ActivationFunctionType
ALU = mybir.AluOpType
AX = mybir.AxisListType

ENGS = ["sync", "scalar", "gpsimd", "vector", "tensor"]
STRICT = True
import os
VAR = os.environ.get("K_VAR", "")
S = 2048
NT = 16
EPS = 1e-6
PI = float(np.pi)


class Op:
    __slots__ = ("eng", "fn", "dma", "semkey", "deps", "signal", "sigval", "idx")


class Prog:
    def __init__(self, nc, ctx):
        self.nc = nc
        self.ctx = ctx
        self.ops = {e: [] for e in ENGS}
        self.last_w = {}
        self.readers = {}
        self.dma_count = {}
        self.waitall = set()
        self.outkeys = set()
        self.n = 0
        self.regmap = {}
        self.region_ops = {}
        self.pending = {}
        self.frozen = False
        self.strict = STRICT

    def declare(self, name, regions):
        self.regmap[name] = regions

    def handoff(self, region):
        self.pending[region] = list(self.region_ops.get(region, {}).values())
        self.region_ops[region] = {}

    def add(self, eng, fn, r=(), w=(), dma=False, semkey=None, waitall=False, out=False):
        if self.frozen:
            return None
        op = Op()
        op.eng, op.fn, op.dma, op.semkey = eng, fn, dma, semkey
        op.signal = False
        op.sigval = None
        op.idx = self.n
        self.n += 1
        deps = {}
        regs = set()
        for k in list(r) + list(w):
            name = k[0] if isinstance(k, tuple) else k
            for rg in self.regmap.get(name, ()):
                regs.add(rg)
        for rg in regs:
            for d in self.pending.get(rg, ()):
                deps[d] = "raw"
        for k in r:
            d = self.last_w.get(k)
            if d is not None:
                deps[d] = "raw"
        for k in w:
            d = self.last_w.get(k)
            if d is not None and d not in deps:
                deps[d] = "waw"
            lastrd = {}
            for rd in self.readers.get(k, ()):
                if rd.dma:
                    lastrd[id(rd)] = rd
                else:
                    lastrd[rd.eng] = rd
            for rd in lastrd.values():
                if rd not in deps:
                    deps[rd] = "war"
        need = []
        for d, kind in deps.items():
            if d is op:
                continue
            if d.dma:
                need.append(d)
            elif d.eng == eng and not dma:
                if eng == "tensor":
                    continue
                if kind == "raw" or self.strict:
                    need.append(d)
            else:
                need.append(d)
        for d in need:
            d.signal = True
        op.deps = need
        for k in r:
            self.readers.setdefault(k, []).append(op)
        for k in w:
            self.last_w[k] = op
            self.readers[k] = []
        if dma:
            c = self.dma_count.get(semkey, 0) + 1
            self.dma_count[semkey] = c
            op.sigval = 16 * c
            if waitall:
                self.waitall.add(semkey)
            if out:
                self.outkeys.add(semkey)
        for rg in regs:
            ro = self.region_ops.setdefault(rg, {})
            ro[("d", semkey) if dma else eng] = op
        self.ops[eng].append(op)
        return op

    def emit(self):
        nc = self.nc
        ctx = self.ctx
        esem = {e: ctx.enter_context(nc.semaphore("s_" + e)) for e in ENGS}
        dsem = {}
        for i, k in enumerate(self.dma_count):
            dsem[k] = ctx.enter_context(nc.semaphore("d%d" % i))
        for e in ENGS:
            c = 0
            for op in self.ops[e]:
                if not op.dma and op.signal:
                    c += 1
                    op.sigval = c
        self.nwaits = 0

        def emit_engine(eng_name):
            def body(eng):
                waited = {}
                for op in self.ops[eng_name]:
                    wl = {}
                    for d in op.deps:
                        if d.dma:
                            s = dsem[d.semkey]
                            v = 16 * self.dma_count[d.semkey] if d.semkey in self.waitall else d.sigval
                        else:
                            s = esem[d.eng]
                            v = d.sigval
                        key = id(s)
                        if v > wl.get(key, (None, 0))[1]:
                            wl[key] = (s, v)
                    for key, (s, v) in wl.items():
                        if waited.get(key, 0) >= v:
                            continue
                        eng.wait_ge(s, v)
                        self.nwaits += 1
                        waited[key] = v
                    ins = op.fn(eng)
                    if op.dma:
                        ins.then_inc(dsem[op.semkey], 16)
                    elif op.signal:
                        ins.then_inc(esem[eng_name], 1)
                if eng_name == "sync":
                    for k in self.outkeys:
                        eng.wait_ge(dsem[k], 16 * self.dma_count[k])
            return body

        with nc.Block() as block:
            block.sync(emit_engine("sync"))
            block.scalar(emit_engine("scalar"))
            block.gpsimd(emit_engine("gpsimd"))
            block.vector(emit_engine("vector"))
            block.tensor(emit_engine("tensor"))


def bc_last(ap, n):
    a = ap.ap
    return bass.AP(ap.tensor, ap.offset, [list(x) for x in a] + [[0, n]])


def bc_mid(ap, n):
    a = ap.ap
    return bass.AP(ap.tensor, ap.offset, [list(a[0]), [0, n]] + [list(x) for x in a[1:]])


class StopBuild(Exception):
    pass


class Builder:
    def __init__(self, debug=(), stop=None):
        self.debug = set(debug)
        self.taps = {}
        self.stop = stop

    def ck(self, name):
        if self.stop == name:
            self.P.frozen = True

    def mm(self, out, lhsT, rhs, start, stop, r, w, skip=False):
        if skip:
            self.P.add("tensor", lambda e: e.matmul(out, lhsT=lhsT, rhs=rhs, start=start, stop=stop, skip_group_check=True), r=r, w=w)
        else:
            self.P.add("tensor", lambda e: e.matmul(out, lhsT=lhsT, rhs=rhs, start=start, stop=stop), r=r, w=w)

    def tr(self, out, in_, r, w):
        ident = self.ident[:]
        self.P.add("tensor", lambda e: e.transpose(out, in_, ident), r=list(r) + ["ident"], w=w)

    def act(self, out, in_, func, r, w, bias=None, scale=1.0, accum=None):
        kw = {}
        if bias is not None:
            kw["bias"] = bias
        if accum is not None:
            kw["accum_out"] = accum
        self.P.add("scalar", lambda e: e.activation(out, in_, func, scale=scale, **kw), r=r, w=w)

    def ts(self, eng, out, in0, s1, s2, op0, op1, r, w):
        if op1 is None:
            self.P.add(eng, lambda e: e.tensor_scalar(out, in0, s1, None, op0), r=r, w=w)
        else:
            self.P.add(eng, lambda e: e.tensor_scalar(out, in0, s1, s2, op0, op1), r=r, w=w)

    def tt(self, eng, out, in0, in1, op, r, w):
        self.P.add(eng, lambda e: e.tensor_tensor(out, in0, in1, op), r=r, w=w)

    def stt(self, eng, out, in0, scalar, in1, op0, op1, r, w):
        self.P.add(eng, lambda e: e.scalar_tensor_tensor(out, in0, scalar, in1, op0, op1), r=r, w=w)

    def cp(self, eng, out, in_, r, w):
        if eng == "scalar":
            self.P.add(eng, lambda e: e.copy(out, in_), r=r, w=w)
        else:
            self.P.add(eng, lambda e: e.tensor_copy(out, in_), r=r, w=w)

    def dma(self, eng, out, in_, r, w, semkey, waitall=False, outflag=False):
        self.P.add(eng, lambda e: e.dma_start(out=out, in_=in_), r=r, w=w, dma=True, semkey=semkey,
                   waitall=waitall, out=outflag)

    def cload(self, out, in_, key, eng="sync"):
        self.dma(eng, out, in_, [], [key], "const", waitall=True)

    def tap(self, name, ap, key):
        if name not in self.debug:
            return
        shape = list(ap.shape)
        d = self.nc.dram_tensor("dbg_" + name, shape, ap.dtype, kind="ExternalOutput").ap()
        self.taps[name] = "dbg_" + name
        self.dma("sync", d, ap, list(key), [], ("dbg", name), outflag=True)

    def view(self, off, nbytes, dt, pattern=None, **kw):
        assert off % 4 == 0 and nbytes % 4 == 0
        v = self.arena[:, off // 4:(off + nbytes) // 4]
        if dt != F32:
            v = v.bitcast(dt)
        if pattern:
            v = v.rearrange(pattern, **kw)
        return v

    def cview(self, ncols, pattern=None, **kw):
        o = self.coff
        self.coff += ncols
        v = self.cst[:, o:o + ncols]
        if pattern:
            v = v.rearrange(pattern, **kw)
        return v

    def wfm_load(self, src):
        i = self.wfm_i % 4
        self.wfm_i += 1
        v = self.wfm[i]
        key = ("wfm", i)
        self.dma("gpsimd", v, src, [], [key], key)
        return v, key

    def wtm_load(self, src, view=None):
        i = self.wtm_i % 2
        self.wtm_i += 1
        v = self.wtm[i] if view is None else view(i)
        key = ("wtm", i)
        self.dma("gpsimd", v, src, [], [key], key)
        return v, key

    def build(self):
        nc = bass.Bass("TRN2", target_bir_lowering=False)
        self.nc = nc
        din = lambda name, shape, dt=F32: nc.dram_tensor(name, shape, dt, kind="ExternalInput").ap()
        D = {}
        D["x"] = din("x", [2, S, 1024])
        D["cT"] = din("cT", [128, 8, 2])
        D["posT"] = din("posT", [128, 2, 16], I32)
        D["w_ada"] = din("w_ada", [48, 128, 8, 128])
        D["b_ada"] = din("b_ada", [128, 48])
        D["n1g"] = din("n1g", [128, 8])
        D["n2g"] = din("n2g", [128, 8])
        D["fing"] = din("fing", [1024])
        D["w_in_fm"] = din("w_in_fm", [32, 128, 8, 128])
        D["w_in_tm"] = din("w_in_tm", [3, 2, 128, 8, 512])
        D["crw"] = din("crw", [128, 8, 4])
        D["crb"] = din("crb", [128, 8])
        D["w_rg"] = din("w_rg", [2, 2, 16, 64, 64])
        D["brg"] = din("brg", [128, 2, 2, 8])
        D["lamrg"] = din("lamrg", [128, 2, 8])
        D["w_rnn_o"] = din("w_rnn_o", [8, 128, 8, 128])
        D["w_attn_o"] = din("w_attn_o", [8, 128, 8, 128])
        D["w_out"] = din("w_out", [2, 128, 8, 512])
        D["lamv"] = din("lamv", [4, 64])
        D["subg"] = din("subg", [128, 1])
        D["w_up"] = din("w_up", [44, 128, 8, 128])
        D["cfw"] = din("cfw", [128, 44, 3])
        D["cfb"] = din("cfb", [128, 44])
        D["w_down"] = din("w_down", [22, 128, 1024])
        self.D = D
        out_d = nc.dram_tensor("out", [2, S, 1024], F32, kind="ExternalOutput").ap()

        with ExitStack() as ctx:
            sb = lambda name, shape, dt: ctx.enter_context(nc.sbuf_tensor(name, shape, dt))
            ARENA = 187904
            self.arena = sb("arena", [128, ARENA // 4], F32)
            self.cst = sb("cst", [128, 1848], F32)
            self.coff = 0
            self.ident = sb("ident", [128, 128], BF16)
            cactb = sb("cactb", [128, 8, 2], BF16)
            rgw = sb("rgw", [128, 2, 2, 8, 128], BF16)
            gb = sb("gb", [128, 1024], F32)
            fingb = sb("fingb", [128, 1024], F32)
            psA = ctx.enter_context(nc.psum_tensor("psA", [128, 2048], F32))
            psB = ctx.enter_context(nc.psum_tensor("psB", [128, 2048], F32))
            P = Prog(nc, ctx)
            self.P = P

            def bank(i):
                t = psA if i < 4 else psB
                j = i % 4
                return t[:, j * 512:(j + 1) * 512]

            def bankbf(i):
                return bank(i).bitcast(BF16)

            pk = lambda i: ("ps", i)

            O_WFM, O_WTM, O_PT, O_SCR = 0, 8192, 24576, 27648
            O_A, O_B, O_C, O_D = 39936, 72704, 105472, 138240
            self.wfm = [self.view(O_WFM + i * 2048, 2048, BF16, "p (k c) -> p k c", c=128) for i in range(4)]
            self.wtm = [self.view(O_WTM + i * 8192, 8192, BF16, "p (k c) -> p k c", c=512) for i in range(2)]
            self.wfm_i = 0
            self.wtm_i = 0
            PT = [self.view(O_PT + i * 1024, 1024, BF16) for i in range(3)]

            def scr(i, dt, pattern=None, **kw):
                return self.view(O_SCR + i * 2048, 2048, dt, pattern, **kw)

            sk = lambda i: ("S", i)
            for nm, rg in [("hT", ["A"]), ("x", ["B", "C"]), ("OT", ["B"]), ("hgT", ["B"]), ("M", ["C"]),
                           ("Dtok", ["C"]), ("qT", ["D"]), ("kT", ["D"]), ("V", ["D"]), ("rgt", ["D"]),
                           ("x1", ["A", "B"]), ("h2T", ["C"]), ("ffn", ["D"])]:
                P.declare(nm, rg)

            hT = self.view(O_A, 32768, BF16, "p (k t) -> p k t", t=S)
            xres = self.view(O_B, 65536, F32, "p (n c) -> p n c", c=1024)
            OT = self.view(O_B, 32768, BF16, "p (k t) -> p k t", t=S)
            hgT = OT
            M = self.view(O_C, 32768, BF16, "p (k t) -> p k t", t=S)
            Dtok = self.view(O_C, 16384, BF16, "p (n c) -> p n c", c=512)
            qT = self.view(O_D, 16384, BF16, "p (h t) -> p h t", t=S)
            kT = self.view(O_D + 16384, 16384, BF16, "p (h t) -> p h t", t=S)
            V = self.view(O_D + 32768, 16640, BF16, "p (n h e) -> p n h e", h=4, e=130)
            x1 = self.view(O_A, 65536, F32, "p (n c) -> p n c", c=1024)
            h2T = self.view(O_C, 32768, BF16, "p (k t) -> p k t", t=S)

            cv = self.cview
            modT = cv(96, "p (k b) -> p k b", b=2)
            bada = cv(48)
            n1g = cv(8)
            n2g = cv(8)
            gs1 = cv(16, "p (k b) -> p k b", b=2)
            gs2 = cv(16, "p (k b) -> p k b", b=2)
            crw = cv(32, "p (k t) -> p k t", t=4)
            crb = cv(8)
            brg = cv(32, "p (g d k) -> p g d k", g=2, d=2)
            bh = cv(32, "p (g d k) -> p g d k", g=2, d=2)
            lamrg = cv(16, "p (d k) -> p d k", d=2)
            sp_ = cv(16, "p (d k) -> p d k", d=2)
            chalf = cv(16, "p (d k) -> p d k", d=2)
            cfull = cv(16, "p (d k) -> p d k", d=2)
            subg = cv(1)
            subgs = cv(1)
            lamq = cv(256, "p (v e) -> p v e", e=64)
            lprod = cv(128, "p (v e) -> p v e", e=64)
            ls = cv(2)
            le = cv(2)
            neglam = cv(1)
            cfw = cv(132, "p (j t) -> p j t", t=3)
            cfb = cv(44)
            ss = cv(16)
            rstd = cv(16)
            ssq = cv(64, "p (n h) -> p n h", h=4)
            rstda = cv(64, "p (n h) -> p n h", h=4)
            posi = cv(32).bitcast(I32).rearrange("p (b n) -> p b n", b=2)
            posf = cv(32, "p (b n) -> p b n", b=2)
            ang = cv(128, "p (n j) -> p n j", j=8)
            kf = cv(128, "p (n j) -> p n j", j=8)
            ki = cv(128).bitcast(I32).rearrange("p (n j) -> p n j", j=8)
            sinT = cv(128, "p (n j) -> p n j", j=8)
            cosT = cv(128, "p (n j) -> p n j", j=8)
            cact32 = cv(16, "p (k b) -> p k b", b=2)
            c1 = cv(1)
            c025 = cv(1)
            ceps = cv(1)
            rec = cv(8)
            gtmp = cv(8)
            ghi32 = cv(8)
            glo32 = cv(8)
            assert self.coff <= 1848, self.coff

            ident = self.ident
            P.add("gpsimd", lambda e: e.memset(ident[:], 0.0), w=["ident"])
            P.add("gpsimd", lambda e: e.affine_select(out=ident[:], in_=ident[:], compare_op=ALU.not_equal, fill=1.0,
                                                      base=0, pattern=[[-1, 128]], channel_multiplier=1),
                  r=["ident"], w=["ident"])
            P.add("gpsimd", lambda e: e.memset(c1, 1.0), w=["c1"])
            P.add("gpsimd", lambda e: e.memset(c025, 0.25), w=["c025"])
            P.add("gpsimd", lambda e: e.memset(ceps, EPS), w=["ceps"])
            rgwk = [("rgw", i) for i in range(8)]
            P.add("gpsimd", lambda e: e.memset(rgw[:], 0.0), w=rgwk)
            self.cload(cact32, D["cT"], "cact32")
            self.cload(posi, D["posT"], "posi")
            self.cload(bada, D["b_ada"], "bada")
            self.cload(n1g, D["n1g"], "n1g")
            self.cload(n2g, D["n2g"], "n2g")
            self.cload(fingb[:], D["fing"].partition_broadcast(128), "fingb")
            self.cload(crw, D["crw"], "crw")
            self.cload(crb, D["crb"], "crb")
            self.cload(brg, D["brg"], "brg")
            self.cload(lamrg, D["lamrg"], "lamrg")
            self.cload(subg, D["subg"], "subg")
            for v_ in range(4):
                self.cload(lamq[:, v_, :], D["lamv"][v_, :].partition_broadcast(128), ("lamq", v_))
            self.cload(cfw, D["cfw"], "cfw")
            self.cload(cfb, D["cfb"], "cfb")
            for g in range(2):
                for d in range(2):
                    for hb in range(2):
                        src = D["w_rg"][g, d].rearrange("(c h) k j -> h k c j", h=2)[hb]
                        dst = rgw[hb * 64:(hb + 1) * 64, g, d, :, hb * 64:(hb + 1) * 64]
                        self.dma("gpsimd", dst, src, [], [("rgw", g * 4 + d * 2 + hb)], "constg", waitall=True)
            self.tap("c_crw", crw, ["crw"])
            self.ck("c1")

            self.act(cactb[:], cact32, AF.Silu, ["cact32"], ["cactb"])
            self.tap("c_cactb", cactb[:], ["cactb"])
            self.tap("c_rgw", rgw[:, 0, 0, 0, :], rgwk)
            self.ck("c2")
            pend = deque()
            for ch in range(2):
                pend.append(self.wfm_load(D["w_ada"][ch]))
            for ch in range(48):
                wv, wk = pend.popleft()
                if ch + 2 < 48:
                    pend.append(self.wfm_load(D["w_ada"][ch + 2]))
                for kc in range(8):
                    self.mm(bank(0)[:, ch * 2:ch * 2 + 2], wv[:, kc, :], cactb[:, kc, :], kc == 0, kc == 7,
                            [wk, "cactb"], [pk(0)])
            self.tt("vector", modT, bank(0)[:, 0:96].rearrange("p (k b) -> p k b", b=2), bc_last(bada, 2), ALU.add,
                    [pk(0), "bada"], ["modT"])
            self.tap("c_modT", modT, ["modT"])
            self.ck("c3")
            for b in range(2):
                self.stt("vector", gs1[:, :, b], modT[:, 8:16, b], 1.0, n1g, ALU.add, ALU.mult, ["modT", "n1g"], [("gs1", b)])
                self.stt("vector", gs2[:, :, b], modT[:, 32:40, b], 1.0, n2g, ALU.add, ALU.mult, ["modT", "n2g"], [("gs2", b)])
            self.act(sp_, lamrg, AF.Exp, ["lamrg"], ["sp"], scale=-1.0)
            self.act(sp_, sp_, AF.Ln, ["sp", "c1"], ["sp"], bias=c1)
            self.ts("vector", chalf, sp_, -4.0, None, ALU.mult, None, ["sp"], ["chalf"])
            self.ts("vector", cfull, sp_, -8.0, None, ALU.mult, None, ["sp"], ["cfull"])
            self.ts("vector", bh, brg, 0.5, None, ALU.mult, None, ["brg"], ["bh"])
            self.tap("c_chalf", chalf, ["chalf"])
            self.ck("c4")
            for i in range(2):
                self.tt("vector", lprod[:, i, :], lamq[:, 2 * i, :], lamq[:, 2 * i + 1, :], ALU.mult,
                        [("lamq", 2 * i), ("lamq", 2 * i + 1)], [("lprod", i)])
                P.add("vector", lambda e, i=i: e.reduce_sum(ls[:, i:i + 1], lprod[:, i, :], axis=AX.X),
                      r=[("lprod", i)], w=[("ls", i)])
            self.act(le, ls, AF.Exp, [("ls", 0), ("ls", 1)], ["le"])
            self.tt("vector", neglam, le[:, 1:2], le[:, 0:1], ALU.subtract, ["le"], ["neglam"])
            self.ts("vector", neglam, neglam, -0.2, None, ALU.add, None, ["neglam"], ["neglam"])
            self.ts("vector", subgs, subg, 0.8, None, ALU.mult, None, ["subg"], ["subgs"])
            self.cp("vector", posf, posi, ["posi"], ["posf"])
            self.tap("modT", modT, ["modT"])

            invf = (np.float32(500000.0) ** (-np.arange(0, 16, 2, dtype=np.float32) / np.float32(16))).astype(np.float32)

            def range_reduce_sin(dst):
                self.ts("vector", kf, ang, 1.0 / (2 * PI), None, ALU.mult, None, ["ang"], ["kf"])
                self.cp("vector", ki, kf, ["kf"], ["ki"])
                self.cp("vector", kf, ki, ["ki"], ["kf"])
                self.stt("vector", ang, kf, -2 * PI, ang, ALU.mult, ALU.add, ["kf", "ang"], ["ang"])
                self.ts("vector", kf, ang, PI, -2 * PI, ALU.is_gt, ALU.mult, ["ang"], ["kf"])
                self.tt("vector", ang, ang, kf, ALU.add, ["ang", "kf"], ["ang"])
                self.ts("vector", kf, ang, -PI, 2 * PI, ALU.is_lt, ALU.mult, ["ang"], ["kf"])
                self.tt("vector", ang, ang, kf, ALU.add, ["ang", "kf"], ["ang"])
                self.act(dst, ang, AF.Sin, ["ang"], [("rope", 0)])

            def gate_row(b, chunk0, scale):
                gcol = modT[:, chunk0:chunk0 + 8, b]
                self.ts("vector", gtmp, gcol, scale, None, ALU.mult, None, ["modT"], ["gtmp"])
                GH = scr(0, BF16, "p (k c) -> p k c", c=128)
                GL = scr(1, BF16, "p (k c) -> p k c", c=128)
                self.cp("vector", GH, bc_last(gtmp, 128), ["gtmp"], [sk(0)])
                self.cp("vector", ghi32, GH[:, :, 0], [sk(0)], ["ghi32"])
                self.tt("vector", glo32, gtmp, ghi32, ALU.subtract, ["gtmp", "ghi32"], ["glo32"])
                self.cp("vector", GL, bc_last(glo32, 128), ["glo32"], [sk(1)])
                for kc in range(8):
                    bk = kc // 4
                    o = bank(bk)[:, (kc % 4) * 128:(kc % 4 + 1) * 128]
                    self.mm(o, GH[:, kc, :], ident[:], True, False, [sk(0), "ident"], [pk(bk)])
                    self.mm(o, GL[:, kc, :], ident[:], False, True, [sk(1), "ident"], [pk(bk)])
                for bk in range(2):
                    self.cp("vector", gb[:, bk * 512:(bk + 1) * 512], bank(bk), [pk(bk)], ["gb"])

            try:
                self.body(locals())
            except StopBuild:
                pass
            P.emit()
        return nc

    def body(self, L):
        globals_ = L
        (P, D, hT, xres, OT, hgT, M, Dtok, qT, kT, V, x1, h2T, modT, gs1, gs2, crw, crb, bh, chalf, cfull, subgs, neglam,
         cfw, cfb, ss, rstd, ssq, rstda, posf, ang, sinT, cosT, c025, ceps, rec, rgw, gb, fingb, psA, psB, bank, bankbf, pk, sk,
         scr, PT, invf, range_reduce_sin, gate_row, out_d, O_SCR, O_D, O_WTM) = [L[k] for k in (
            "P", "D", "hT", "xres", "OT", "hgT", "M", "Dtok", "qT", "kT", "V", "x1", "h2T", "modT", "gs1", "gs2", "crw", "crb",
            "bh", "chalf", "cfull", "subgs", "neglam", "cfw", "cfb", "ss", "rstd", "ssq", "rstda", "posf", "ang", "sinT", "cosT",
            "c025", "ceps", "rec", "rgw", "gb", "fingb", "psA", "psB", "bank", "bankbf", "pk", "sk", "scr", "PT", "invf",
            "range_reduce_sin", "gate_row", "out_d", "O_SCR", "O_D", "O_WTM")]
        if True:
            self.ck("p0")
            for b in range(2):
                for j in range(8):
                    self.ts("vector", ang[:, :, j], posf[:, b, :], float(invf[j]), None, ALU.mult, None, ["posf"], ["ang"])
                range_reduce_sin(sinT)
                for j in range(8):
                    self.ts("vector", ang[:, :, j], posf[:, b, :], float(invf[j]), None, ALU.mult, None, ["posf"], ["ang"])
                self.ts("vector", ang, ang, PI / 2, None, ALU.add, None, ["ang"], ["ang"])
                range_reduce_sin(cosT)

                for rg in ("A", "B", "C"):
                    P.handoff(rg)
                sskeys = [("ss", tt) for tt in range(NT)]
                P.add("vector", lambda e: e.memset(ss, 0.0), w=sskeys)
                for tt in range(NT):
                    self.dma("sync", xres[:, tt, :], D["x"][b, tt * 128:(tt + 1) * 128, :], [], [("x", tt)], ("x", tt))
                    self.act(scr(0, BF16), xres[:, tt, :], AF.Square, [("x", tt)], [sk(0), ("ss", tt)], accum=ss[:, tt:tt + 1])
                sskeys = [("ss", tt) for tt in range(NT)]
                self.ts("vector", rstd, ss, 1.0 / 1024, None, ALU.mult, None, sskeys, ["rstd"])
                self.act(rstd, rstd, AF.Sqrt, ["rstd", "ceps"], ["rstd"], bias=ceps)
                P.add("vector", lambda e: e.reciprocal(rstd, rstd), r=["rstd"], w=["rstd"])
                for tt in range(NT):
                    si = 1 + tt % 2
                    xnb = scr(si, BF16)
                    self.ts("vector", xnb, xres[:, tt, :], rstd[:, tt:tt + 1], None, ALU.mult, None,
                            [("x", tt), "rstd"], [sk(si)])
                    bk = tt % 4
                    pv = bankbf(bk).rearrange("p (k c) -> p k c", c=128)
                    for kc in range(8):
                        self.tr(pv[:, kc, :], xnb[:, kc * 128:(kc + 1) * 128], [sk(si)], [pk(bk)])
                    for kc in range(8):
                        o = hT[:, kc, tt * 128:(tt + 1) * 128]
                        if tt % 2 == 0:
                            self.act(o, pv[:, kc, :], AF.Identity, [pk(bk), ("gs1", b), "modT"], [("hT", tt)],
                                     bias=modT[:, kc, b:b + 1], scale=gs1[:, kc, b:b + 1])
                        else:
                            self.ts("vector", o, pv[:, kc, :], gs1[:, kc, b:b + 1], modT[:, kc, b:b + 1], ALU.mult, ALU.add,
                                    [pk(bk), ("gs1", b), "modT"], [("hT", tt)])
                if b == 0:
                    self.tap("hT", hT, [("hT", tt) for tt in range(NT)])
                    self.ck("p1")
                hTk = lambda tb: [("hT", 4 * tb + i) for i in range(4)]

                for rg in ("B", "C", "D"):
                    P.handoff(rg)
                for hg in range(2):
                    P.add("gpsimd", lambda e: e.memset(V[:, :, :, 128:130], 1.0), w=[("V", tt) for tt in range(NT)])
                    pend = deque()
                    pend.append(self.wtm_load(D["w_in_tm"][0, hg]))
                    for wi in range(3):
                        wv, wk = pend.popleft()
                        if wi + 1 < 3:
                            pend.append(self.wtm_load(D["w_in_tm"][wi + 1, hg]))
                        for tt in range(NT):
                            bk = tt % 2
                            for kc in range(8):
                                self.mm(bank(bk), hT[:, kc, tt * 128:(tt + 1) * 128], wv[:, kc, :], kc == 0, kc == 7,
                                        [("hT", tt), wk], [pk(bk)])
                            if wi == 2:
                                self.cp("scalar", V[:, tt, :, 0:128], bank(bk).rearrange("p (h e) -> p h e", e=128),
                                        [pk(bk)], [("V", tt)])
                                continue
                            si = 3 + tt % 2
                            fi = tt % 2
                            qtok = scr(si, BF16)[:, 0:512]
                            q3 = qtok.rearrange("p (g e) -> p g e", e=64)
                            qf = scr(fi, F32)
                            p3 = qf.rearrange("p (g e) -> p g e", e=64)
                            rt = self.view(O_SCR + 5 * 2048, 1024, F32, "p (a g e) -> p a g e", a=4, g=8)
                            self.cp("scalar", qf, bank(bk), [pk(bk)], [sk(fi)])
                            self.cp("vector", qtok, qf, [sk(fi)], [sk(si)])
                            cs = bc_mid(cosT[:, tt, :], 8)
                            sn = bc_mid(sinT[:, tt, :], 8)
                            rk = [sk(fi), ("rope", 0)]
                            self.tt("vector", rt[:, 0], p3[:, :, 0:8], cs, ALU.mult, rk, [("rt", 0)])
                            self.tt("vector", rt[:, 1], p3[:, :, 8:16], sn, ALU.mult, rk, [("rt", 1)])
                            self.tt("vector", q3[:, :, 0:8], rt[:, 0], rt[:, 1], ALU.subtract, [("rt", 0), ("rt", 1)], [sk(si)])
                            self.tt("vector", rt[:, 2], p3[:, :, 8:16], cs, ALU.mult, rk, [("rt", 2)])
                            self.tt("vector", rt[:, 3], p3[:, :, 0:8], sn, ALU.mult, rk, [("rt", 3)])
                            self.tt("vector", q3[:, :, 8:16], rt[:, 2], rt[:, 3], ALU.add, [("rt", 2), ("rt", 3)], [sk(si)])
                            tb_ = 2 + tt % 2
                            pv = bankbf(tb_)[:, 0:512].rearrange("p (h c) -> p h c", c=128)
                            for hl in range(4):
                                self.tr(pv[:, hl, :], qtok[:, hl * 128:(hl + 1) * 128], [sk(si)], [pk(tb_)])
                            dstT = qT if wi == 0 else kT
                            nm = "qT" if wi == 0 else "kT"
                            self.cp("vector", dstT[:, :, tt * 128:(tt + 1) * 128], pv, [pk(tb_)], [(nm, tt)])
                            if b == 0 and hg == 0 and wi == 0:
                                self.ck("p2t%d" % tt)
                            if b == 0 and hg == 0 and tt == NT - 1:
                                self.tap("w%d" % wi, dstT, [(nm, t_) for t_ in range(NT)])
                                self.ck("p2w%d" % wi)
                            if b == 0 and hg == 0 and wi == 0 and tt == 0:
                                self.tap("qtok0", qtok, [sk(si)])
                                self.tap("qT0", qT[:, :, 0:128], [("qT", 0)])
                                self.ck("p2a0")
                    if b == 0 and hg == 0:
                        self.tap("qT", qT, [("qT", tt) for tt in range(NT)])
                        self.tap("kT", kT, [("kT", tt) for tt in range(NT)])
                        self.tap("V", V, [("V", tt) for tt in range(NT)])
                        self.ck("p2a")
                    LA = 2
                    steps = [(hl, qb, c, kt) for hl in range(4) for qb in range(4) for c in range(2) for kt in range(NT)]
                    NS = len(steps)
                    A0 = scr(0, F32, "p (s e) -> p s e", e=128)
                    T_ = scr(1, F32, "p (s e) -> p s e", e=128)
                    SQ = scr(2, F32, "p (s e) -> p s e", e=128)

                    def emit_qk(i):
                        hl, qb, c, kt = steps[i]
                        sb_ = i % 3
                        qk_ = [("qT", 4 * qb + t_) for t_ in range(4)]
                        self.mm(bank(sb_), kT[c * 64:(c + 1) * 64, hl, kt * 128:(kt + 1) * 128],
                                qT[c * 64:(c + 1) * 64, hl, qb * 512:(qb + 1) * 512], True, True,
                                [("kT", kt)] + qk_, [pk(sb_)])
                        self.act(PT[sb_], bank(sb_), AF.Exp, [pk(sb_)], [("PT", sb_)], scale=0.125)

                    def emit_pv(i):
                        hl, qb, c, kt = steps[i]
                        sb_ = i % 3
                        oacc = psB[:, c * 1024:(c + 1) * 1024].rearrange("p (s w) -> p s w", w=256)
                        ok = [pk(4 + 2 * c), pk(5 + 2 * c)]
                        for s_ in range(4):
                            self.mm(oacc[:, s_, 0:129], PT[sb_][:, s_ * 128:(s_ + 1) * 128], V[:, kt, hl, 0:129],
                                    kt == 0 and s_ % 2 == 0, kt == NT - 1 and s_ % 2 == 1,
                                    [("PT", sb_), ("V", kt)], [ok[s_ // 2]], skip=True)
                        if kt != NT - 1:
                            return
                        rc = rec[:, c * 4:(c + 1) * 4]
                        P.add("vector", lambda e, rc=rc, oacc=oacc: e.reciprocal(rc, oacc[:, :, 128]), r=ok, w=[("rec", c)])
                        if c == 0:
                            self.tt("vector", A0, oacc[:, :, 0:128], bc_last(rc, 128), ALU.mult, ok + [("rec", 0)], [sk(0)])
                        else:
                            self.ts("vector", rc, rc, neglam, None, ALU.mult, None, [("rec", 1), "neglam"], [("rec", 1)])
                            self.tt("vector", T_, oacc[:, :, 0:128], bc_last(rc, 128), ALU.mult, ok + [("rec", 1)], [sk(1)])
                            self.tt("vector", T_, T_, A0, ALU.add, [sk(0), sk(1)], [sk(1)])
                            dk = [("Dtok", 4 * qb + t_) for t_ in range(4)]
                            self.cp("gpsimd", Dtok[:, 4 * qb:4 * qb + 4, hl * 128:(hl + 1) * 128], T_, [sk(1)], dk)
                            self.tt("vector", SQ, T_, T_, ALU.mult, [sk(1)], [sk(2)])
                            P.add("vector", lambda e, qb=qb, hl=hl: e.reduce_sum(ssq[:, 4 * qb:4 * qb + 4, hl], SQ, axis=AX.X),
                                  r=[sk(2)], w=[("ssq", qb, hl)])

                    for i in range(NS + LA):
                        if i < NS:
                            emit_qk(i)
                        if i - LA >= 0:
                            emit_pv(i - LA)
                    sqk = [("ssq", qb, hl) for qb in range(4) for hl in range(4)]
                    self.ts("vector", rstda, ssq, 1.0 / 128, None, ALU.mult, None, sqk, ["rstda"])
                    self.act(rstda, rstda, AF.Sqrt, ["rstda", "ceps"], ["rstda"], bias=ceps)
                    P.add("vector", lambda e: e.reciprocal(rstda, rstda), r=["rstda"], w=["rstda"])
                    for tt in range(NT):
                        si = 3 + tt % 2
                        Dn = scr(si, BF16)[:, 0:512]
                        self.tt("vector", Dn.rearrange("p (h e) -> p h e", e=128),
                                Dtok[:, tt, :].rearrange("p (h e) -> p h e", e=128), bc_last(rstda[:, tt, :], 128), ALU.mult,
                                [("Dtok", tt), "rstda"], [sk(si)])
                        bk = tt % 2
                        pv = bankbf(bk)[:, 0:512].rearrange("p (h c) -> p h c", c=128)
                        for hl in range(4):
                            self.tr(pv[:, hl, :], Dn[:, hl * 128:(hl + 1) * 128], [sk(si)], [pk(bk)])
                        self.ts("vector", OT[:, hg * 4:(hg + 1) * 4, tt * 128:(tt + 1) * 128], pv, subgs, None, ALU.mult, None,
                                [pk(bk), "subgs"], [("OT", tt)])
                if b == 0:
                    self.tap("OT", OT, [("OT", tt) for tt in range(NT)])
                    self.ck("p2")
                OTk = lambda tb: [("OT", 4 * tb + i) for i in range(4)]

                def merge(gate_ch0, wsrc, srcT, srck, first):
                    pend = deque()
                    pend.append((self.wfm_load(D["w_in_fm"][gate_ch0]), self.wfm_load(wsrc[0])))
                    for j in range(8):
                        (wg, wgk), (wo, wok) = pend.popleft()
                        if j + 1 < 8:
                            pend.append((self.wfm_load(D["w_in_fm"][gate_ch0 + j + 1]), self.wfm_load(wsrc[j + 1])))
                        for tb in range(4):
                            bx, by = 2 * (tb % 2), 2 * (tb % 2) + 1
                            cols = slice(tb * 512, (tb + 1) * 512)
                            for kc in range(8):
                                self.mm(bank(bx), wg[:, kc, :], hT[:, kc, cols], kc == 0, kc == 7, [wgk] + hTk(tb), [pk(bx)])
                            for kc in range(8):
                                self.mm(bank(by), wo[:, kc, :], srcT[:, kc, cols], kc == 0, kc == 7, [wok] + srck(tb), [pk(by)])
                            si = tb % 2
                            tg = scr(si, F32)
                            self.act(tg, bank(bx), AF.Tanh, [pk(bx)], [sk(si)], scale=0.5)
                            mk = [("M", j, tb)]
                            if first:
                                self.stt("vector", M[:, j, cols], tg, 1.0, bank(by), ALU.add, ALU.mult, [sk(si), pk(by)], mk)
                            else:
                                mt = scr(2 + si, F32)
                                self.stt("vector", mt, tg, 1.0, bank(by), ALU.add, ALU.mult, [sk(si), pk(by)], [sk(2 + si)])
                                self.tt("gpsimd", M[:, j, cols], mt, M[:, j, cols], ALU.add, [sk(2 + si)] + mk, mk)

                P.handoff("C")
                merge(24, D["w_attn_o"], OT, OTk, True)
                if b == 0:
                    self.ck("p3")

                P.handoff("B")
                P.handoff("D")
                xc32 = self.view(O_D, 8192, F32)
                xcb = self.view(O_D + 8192, 4096, BF16)
                G = self.view(O_D + 12288, 4096, BF16)
                T2 = self.view(O_D + 16384, 8192, F32)
                T3 = self.view(O_D + 24576, 8192, F32)
                Aa = [self.view(O_D + 32768, 8192, F32), self.view(O_D + 40960, 8192, F32)]
                pAk = [pk(i) for i in range(4)]
                pBk = [pk(4 + i) for i in range(4)]
                allh = [("hT", tt) for tt in range(NT)]
                pend = deque()
                pend.append((self.wfm_load(D["w_in_fm"][0]), self.wfm_load(D["w_in_fm"][8])))
                for c in range(8):
                    (wx, wxk), (wy, wyk) = pend.popleft()
                    if c + 1 < 8:
                        pend.append((self.wfm_load(D["w_in_fm"][c + 1]), self.wfm_load(D["w_in_fm"][8 + c + 1])))
                    for kc in range(8):
                        for tb in range(4):
                            self.mm(bank(tb), wx[:, kc, :], hT[:, kc, tb * 512:(tb + 1) * 512], kc == 0, kc == 7,
                                    [wxk] + hTk(tb), [pk(tb)])
                    for kc in range(8):
                        for tb in range(4):
                            self.mm(bank(4 + tb), wy[:, kc, :], hT[:, kc, tb * 512:(tb + 1) * 512], kc == 0, kc == 7,
                                    [wyk] + hTk(tb), [pk(4 + tb)])
                    self.act(xc32, psA[:, :], AF.Identity, pAk + ["crw", "crb"], [("rgt", "xc")], bias=crb[:, c:c + 1], scale=crw[:, c, 2:3])
                    self.stt("vector", xc32[:, 2:S], psA[:, 0:S - 2], crw[:, c, 0:1], xc32[:, 2:S], ALU.mult, ALU.add,
                             pAk + [("rgt", "xc"), "crw"], [("rgt", "xc")])
                    self.stt("vector", xc32[:, 1:S], psA[:, 0:S - 1], crw[:, c, 1:2], xc32[:, 1:S], ALU.mult, ALU.add,
                             pAk + [("rgt", "xc"), "crw"], [("rgt", "xc")])
                    self.stt("vector", xc32[:, 0:S - 1], psA[:, 1:S], crw[:, c, 3:4], xc32[:, 0:S - 1], ALU.mult, ALU.add,
                             pAk + [("rgt", "xc"), "crw"], [("rgt", "xc")])
                    self.cp("gpsimd", xcb, xc32, [("rgt", "xc")], [("rgt", "xcb")])
                    self.act(T2, psB[:, :], AF.Square, pBk, [("rgt", "T2")])
                    self.ts("vector", T2, T2, 0.044715, 1.0, ALU.mult, ALU.add, [("rgt", "T2")], [("rgt", "T2")])
                    self.tt("vector", T2, psB[:, :], T2, ALU.mult, pBk + [("rgt", "T2")], [("rgt", "T2")])
                    self.act(T2, T2, AF.Tanh, [("rgt", "T2")], [("rgt", "T2")], scale=0.7978845608028654)
                    self.stt("vector", G, T2, 1.0, psB[:, :], ALU.add, ALU.mult, pBk + [("rgt", "T2")], [("rgt", "G")])
                    for d in range(2):
                        A_ = Aa[d]
                        ak = ("rgt", "A", d)
                        for tb in range(4):
                            self.mm(bank(tb), rgw[:, 0, d, c, :], xcb[:, tb * 512:(tb + 1) * 512], True, True,
                                    L["rgwk"] + [("rgt", "xcb")], [pk(tb)])
                        for tb in range(4):
                            self.mm(bank(4 + tb), rgw[:, 1, d, c, :], xcb[:, tb * 512:(tb + 1) * 512], True, True,
                                    L["rgwk"] + [("rgt", "xcb")], [pk(4 + tb)])
                        self.act(T2, psA[:, :], AF.Tanh, pAk + ["bh"], [("rgt", "T2")], bias=bh[:, 0, d, c:c + 1], scale=0.5)
                        self.act(T3, psB[:, :], AF.Tanh, pBk + ["bh"], [("rgt", "T3")], bias=bh[:, 1, d, c:c + 1], scale=0.5)
                        self.act(A_, T2, AF.Exp, [("rgt", "T2"), "chalf"], [ak], bias=chalf[:, d, c:c + 1], scale=chalf[:, d, c:c + 1])
                        self.act(T2, T2, AF.Exp, [("rgt", "T2"), "cfull"], [("rgt", "T2")], bias=cfull[:, d, c:c + 1], scale=cfull[:, d, c:c + 1])
                        self.act(T2, T2, AF.Sqrt, [("rgt", "T2"), "c025"], [("rgt", "T2")], bias=c025, scale=-0.25)
                        self.stt("vector", T3, T3, 1.0, xc32, ALU.add, ALU.mult, [("rgt", "T3"), ("rgt", "xc")], [("rgt", "T3")])
                        self.tt("vector", T3, T3, T2, ALU.mult, [("rgt", "T3"), ("rgt", "T2")], [("rgt", "T3")])
                        if d == 0:
                            P.add("vector", lambda e, A_=A_: e.tensor_tensor_scan(A_, A_, T3, 0.0, ALU.mult, ALU.add),
                                  r=[ak, ("rgt", "T3")], w=[ak])
                        else:
                            P.add("vector", lambda e, A_=A_: e.tensor_tensor_scan(A_[:, ::-1], A_[:, ::-1], T3[:, ::-1], 0.0, ALU.mult, ALU.add),
                                  r=[ak, ("rgt", "T3")], w=[ak])
                    self.tt("gpsimd", Aa[0], Aa[0], Aa[1], ALU.add, [("rgt", "A", 0), ("rgt", "A", 1)], [("rgt", "A", 0)])
                    self.stt("vector", hgT[:, c, :], Aa[0], 0.5, G, ALU.mult, ALU.mult, [("rgt", "A", 0), ("rgt", "G")],
                             [("hgT", c)])
                if b == 0:
                    self.tap("hgT", hgT, [("hgT", c) for c in range(8)])
                    self.ck("p4")
                hgk = lambda tb: [("hgT", c) for c in range(8)]

                merge(16, D["w_rnn_o"], hgT, hgk, False)
                if b == 0:
                    self.tap("M", M, [("M", j, tb) for j in range(8) for tb in range(4)])
                    self.ck("p5")

                gate_row(b, 16, 0.5)
                P.handoff("A")
                P.handoff("B")
                P.add("vector", lambda e: e.memset(ss, 0.0), w=sskeys)
                wo_ = []
                for g in range(2):
                    wo_.append(self.wtm_load(D["w_out"][g]))
                for tt in range(NT):
                    self.dma("sync", x1[:, tt, :], D["x"][b, tt * 128:(tt + 1) * 128, :], [], [("x1", tt)], ("x", tt))
                for tt in range(NT):
                    mkeys = [("M", j, tt // 4) for j in range(8)]
                    for g in range(2):
                        bk = (2 * tt + g) % 4
                        wv, wk = wo_[g]
                        for kc in range(8):
                            self.mm(bank(bk), M[:, kc, tt * 128:(tt + 1) * 128], wv[:, kc, :], kc == 0, kc == 7, mkeys + [wk], [pk(bk)])
                        si = (2 * tt + g) % 2
                        tmp = scr(si, F32)
                        cols = slice(g * 512, (g + 1) * 512)
                        self.tt("vector", tmp, bank(bk), gb[:, cols], ALU.mult, [pk(bk), "gb"], [sk(si)])
                        self.tt("gpsimd", x1[:, tt, cols], tmp, x1[:, tt, cols], ALU.add, [sk(si), ("x1", tt)], [("x1", tt)])
                    self.act(scr(2, BF16), x1[:, tt, :], AF.Square, [("x1", tt)], [sk(2), ("ss", tt)], accum=ss[:, tt:tt + 1])
                if b == 0:
                    self.tap("x1", x1, [("x1", tt) for tt in range(NT)])
                    self.ck("p6")

                P.handoff("C")
                P.handoff("D")
                self.ts("vector", rstd, ss, 1.0 / 1024, None, ALU.mult, None, sskeys, ["rstd"])
                self.act(rstd, rstd, AF.Sqrt, ["rstd", "ceps"], ["rstd"], bias=ceps)
                P.add("vector", lambda e: e.reciprocal(rstd, rstd), r=["rstd"], w=["rstd"])
                for tt in range(NT):
                    si = 3 + tt % 2
                    xnb = scr(si, BF16)
                    self.ts("vector", xnb, x1[:, tt, :], rstd[:, tt:tt + 1], None, ALU.mult, None, [("x1", tt), "rstd"], [sk(si)])
                    bk = tt % 4
                    pv = bankbf(bk).rearrange("p (k c) -> p k c", c=128)
                    for kc in range(8):
                        self.tr(pv[:, kc, :], xnb[:, kc * 128:(kc + 1) * 128], [sk(si)], [pk(bk)])
                    for kc in range(8):
                        o = h2T[:, kc, tt * 128:(tt + 1) * 128]
                        if tt % 2 == 0:
                            self.act(o, pv[:, kc, :], AF.Identity, [pk(bk), ("gs2", b), "modT"], [("h2T", tt)],
                                     bias=modT[:, 24 + kc, b:b + 1], scale=gs2[:, kc, b:b + 1])
                        else:
                            self.ts("vector", o, pv[:, kc, :], gs2[:, kc, b:b + 1], modT[:, 24 + kc, b:b + 1], ALU.mult, ALU.add,
                                    [pk(bk), ("gs2", b), "modT"], [("h2T", tt)])
                gate_row(b, 40, 1.0)
                h2k = lambda tb: [("h2T", 4 * tb + i) for i in range(4)]
                aT = self.view(O_D, 24576, BF16, "p (j t) -> p j t", t=S)
                Yv = self.view(O_D + 24576, 8192, F32)
                Yg = self.view(O_D + 32768, 8192, F32)
                wdn = self.view(O_WTM, 16384, BF16, "p (k c) -> p k c", c=1024)
                groups = [(0, 6), (6, 6), (12, 5), (17, 5)]
                for (j0, nj) in groups:
                    self.dma("gpsimd", wdn[:, 0:nj, :], D["w_down"][j0:j0 + nj].rearrange("k p c -> p k c"), [],
                             [("wtm", 0), ("wtm", 1)], ("wtm", 0))
                    pend = deque()
                    pend.append((self.wfm_load(D["w_up"][j0]), self.wfm_load(D["w_up"][22 + j0])))
                    for jl in range(nj):
                        j = j0 + jl
                        (wv_, wvk), (wg_, wgk) = pend.popleft()
                        if jl + 1 < nj:
                            pend.append((self.wfm_load(D["w_up"][j + 1]), self.wfm_load(D["w_up"][22 + j + 1])))
                        for kc in range(8):
                            for tb in range(4):
                                self.mm(bank(tb), wv_[:, kc, :], h2T[:, kc, tb * 512:(tb + 1) * 512], kc == 0, kc == 7,
                                        [wvk] + h2k(tb), [pk(tb)])
                        for kc in range(8):
                            for tb in range(4):
                                self.mm(bank(4 + tb), wg_[:, kc, :], h2T[:, kc, tb * 512:(tb + 1) * 512], kc == 0, kc == 7,
                                        [wgk] + h2k(tb), [pk(4 + tb)])
                        for (Y, ps_, pks, jj, yk) in ((Yv, psA, pAk, j, ("ffn", "Yv")), (Yg, psB, pBk, 22 + j, ("ffn", "Yg"))):
                            self.act(Y, ps_[:, :], AF.Identity, pks + ["cfw", "cfb"], [yk], bias=cfb[:, jj:jj + 1], scale=cfw[:, jj, 1:2])
                            self.stt("vector", Y[:, 1:S], ps_[:, 0:S - 1], cfw[:, jj, 0:1], Y[:, 1:S], ALU.mult, ALU.add,
                                     pks + [yk, "cfw"], [yk])
                            self.stt("vector", Y[:, 0:S - 1], ps_[:, 1:S], cfw[:, jj, 2:3], Y[:, 0:S - 1], ALU.mult, ALU.add,
                                     pks + [yk, "cfw"], [yk])
                        self.act(Yg, Yg, AF.Silu, [("ffn", "Yg")], [("ffn", "Yg")])
                        self.tt("vector", aT[:, jl, :], Yg, Yv, ALU.mult, [("ffn", "Yg"), ("ffn", "Yv")], [("ffn", "aT", jl)])
                    if b == 0 and j0 == 0:
                        self.tap("aT", aT, [("ffn", "aT", jl) for jl in range(6)])
                        self.ck("p7")
                    ak_ = [("ffn", "aT", jl) for jl in range(nj)]
                    for tt in range(NT):
                        for g in range(2):
                            bk = (2 * tt + g) % 4
                            cols = slice(g * 512, (g + 1) * 512)
                            for jl in range(nj):
                                self.mm(bank(bk), aT[:, jl, tt * 128:(tt + 1) * 128], wdn[:, jl, cols], jl == 0, jl == nj - 1,
                                        ak_ + [("wtm", 0), ("wtm", 1)], [pk(bk)])
                            si = (2 * tt + g) % 2
                            tmp = scr(si, F32)
                            self.tt("vector", tmp, bank(bk), gb[:, cols], ALU.mult, [pk(bk), "gb"], [sk(si)])
                            self.tt("gpsimd", x1[:, tt, cols], tmp, x1[:, tt, cols], ALU.add, [sk(si), ("x1", tt)], [("x1", tt)])
                P.add("vector", lambda e: e.memset(ss, 0.0), w=sskeys)
                for tt in range(NT):
                    self.act(scr(2, BF16), x1[:, tt, :], AF.Square, [("x1", tt)], [sk(2), ("ss", tt)], accum=ss[:, tt:tt + 1])
                self.ts("vector", rstd, ss, 1.0 / 1024, None, ALU.mult, None, sskeys, ["rstd"])
                self.act(rstd, rstd, AF.Sqrt, ["rstd", "ceps"], ["rstd"], bias=ceps)
                P.add("vector", lambda e: e.reciprocal(rstd, rstd), r=["rstd"], w=["rstd"])
                for tt in range(NT):
                    self.stt("vector", x1[:, tt, :], x1[:, tt, :], rstd[:, tt:tt + 1], fingb[:], ALU.mult, ALU.mult,
                             [("x1", tt), "rstd", "fingb"], [("x1", tt)])
                    self.dma("sync", out_d[b, tt * 128:(tt + 1) * 128, :], x1[:, tt, :], [("x1", tt)], [], ("x", tt), outflag=True)


def fm_layout(w, nch):
    return np.ascontiguousarray(w.reshape(8, 128, nch, 128).transpose(2, 1, 0, 3))


def tm_layout(w, ng):
    return np.ascontiguousarray(w.reshape(8, 128, ng, 512).transpose(2, 1, 0, 3))


def pcol(v, n):
    return np.ascontiguousarray(v.reshape(n, 128).T)


_NC_CACHE = {}


def prepare_shared(inp):
    sh = {}
    sh["w_ada"] = fm_layout(inp["w_ada"][0], 48)
    sh["b_ada"] = pcol(inp["b_ada"][0], 48)
    sh["n1g"] = pcol(inp["norm1_g"][0], 8)
    sh["n2g"] = pcol(inp["norm2_g"][0], 8)
    sh["fing"] = np.ascontiguousarray(inp["final_g"])
    w_in = inp["w_in"][0]
    sh["w_in_fm"] = np.concatenate([fm_layout(w_in[:, 0:2048], 16), fm_layout(w_in[:, 5120:7168], 16)], axis=0)
    sh["w_in_tm"] = np.stack([tm_layout(w_in[:, 2048:3072], 2), tm_layout(w_in[:, 3072:4096], 2),
                              tm_layout(w_in[:, 4096:5120], 2)], axis=0)
    sh["crw"] = np.ascontiguousarray(inp["conv_rnn_w"][0].reshape(4, 8, 128).transpose(2, 1, 0))
    sh["crb"] = pcol(inp["conv_rnn_b"][0], 8)
    sh["w_rg"] = np.ascontiguousarray(np.stack([inp["w_rg_a"][0], inp["w_rg_i"][0]], axis=0))
    brg = np.stack([inp["b_rg_a"][0], inp["b_rg_i"][0]], axis=0)
    sh["brg"] = np.ascontiguousarray(brg.reshape(2, 2, 8, 128).transpose(3, 0, 1, 2))
    sh["lamrg"] = np.ascontiguousarray(inp["rg_lambda"][0].reshape(2, 8, 128).transpose(2, 0, 1))
    sh["w_rnn_o"] = fm_layout(inp["w_rnn_o"][0], 8)
    sh["w_attn_o"] = fm_layout(inp["w_attn_o"][0], 8)
    sh["w_out"] = tm_layout(inp["w_out"][0], 2)
    sh["lamv"] = np.ascontiguousarray(np.stack([inp["lam_q1"][0], inp["lam_k1"][0], inp["lam_q2"][0], inp["lam_k2"][0]], axis=0))
    sh["subg"] = np.ascontiguousarray(inp["subln_g"][0].reshape(128, 1))
    sh["w_up"] = fm_layout(inp["w_up"][0], 44)
    sh["cfw"] = np.ascontiguousarray(inp["conv_ffn_w"][0].reshape(3, 44, 128).transpose(2, 1, 0))
    sh["cfb"] = pcol(inp["conv_ffn_b"][0], 44)
    sh["w_down"] = np.ascontiguousarray(inp["w_down"][0].reshape(22, 128, 1024))
    return {k: np.ascontiguousarray(v, dtype=np.float32) for k, v in sh.items()}


def make_in_maps(inp, n_cores=8):
    sh = prepare_shared(inp)
    maps = []
    for i in range(n_cores):
        m = dict(sh)
        m["x"] = np.ascontiguousarray(inp["x"][2 * i:2 * i + 2], dtype=np.float32)
        c2 = inp["c"][2 * i:2 * i + 2]
        m["cT"] = np.ascontiguousarray(c2.reshape(2, 8, 128).transpose(2, 1, 0), dtype=np.float32)
        p2 = inp["positions"][2 * i:2 * i + 2]
        m["posT"] = np.ascontiguousarray(p2.reshape(2, 16, 128).transpose(2, 0, 1), dtype=np.int32)
        maps.append(m)
    return maps


def kernel(**inputs):
    inp = {k: np.asarray(v) for k, v in inputs.items()}
    if "nc" not in _NC_CACHE:
        _NC_CACHE["nc"] = Builder().build()
    nc = _NC_CACHE["nc"]
    maps = make_in_maps(inp, 8)
    res = run_bass_kernel_spmd(nc, maps, core_ids=list(range(8)))
    out = np.concatenate([r["out"] for r in res.results], axis=0)
    return out.astype(np.float32)
```

```python
import numpy as np
from collections import deque
from contextlib import ExitStack
import concourse.bass as bass
import concourse.mybir as mybir
from concourse.bass_utils import run_bass_kernel_spmd

F32 = mybir.dt.float32
BF16 = mybir.dt.bfloat16
I32 = mybir.dt.int32
AF = mybir.ActivationFunctionType
ALU = mybir.AluOpType
AX = mybir.AxisListType

ENGS = ["sync", "scalar", "gpsimd", "vector", "tensor"]
STRICT = True
import os
VAR = os.environ.get("K_VAR", "")
S = 2048
NT = 16
EPS = 1e-6
PI = float(np.pi)


class Op:
    __slots__ = ("eng", "fn", "dma", "semkey", "deps", "signal", "sigval", "idx")


class Prog:
    def __init__(self, nc, ctx):
        self.nc = nc
        self.ctx = ctx
        self.ops = {e: [] for e in ENGS}
        self.last_w = {}
        self.readers = {}
        self.dma_count = {}
        self.waitall = set()
        self.outkeys = set()
        self.n = 0
        self.regmap = {}
        self.region_ops = {}
        self.pending = {}
        self.frozen = False
        self.strict = STRICT

    def declare(self, name, regions):
        self.regmap[name] = regions

    def handoff(self, region):
        self.pending[region] = list(self.region_ops.get(region, {}).values())
        self.region_ops[region] = {}

    def add(self, eng, fn, r=(), w=(), dma=False, semkey=None, waitall=False, out=False):
        if self.frozen:
            return None
        op = Op()
        op.eng, op.fn, op.dma, op.semkey = eng, fn, dma, semkey
        op.signal = False
        op.sigval = None
        op.idx = self.n
        self.n += 1
        deps = {}
        regs = set()
        for k in list(r) + list(w):
            name = k[0] if isinstance(k, tuple) else k
            for rg in self.regmap.get(name, ()):
                regs.add(rg)
        for rg in regs:
            for d in self.pending.get(rg, ()):
                deps[d] = "raw"
        for k in r:
            d = self.last_w.get(k)
            if d is not None:
                deps[d] = "raw"
        for k in w:
            d = self.last_w.get(k)
            if d is not None and d not in deps:
                deps[d] = "waw"
            lastrd = {}
            for rd in self.readers.get(k, ()):
                if rd.dma:
                    lastrd[id(rd)] = rd
                else:
                    lastrd[rd.eng] = rd
            for rd in lastrd.values():
                if rd not in deps:
                    deps[rd] = "war"
        need = []
        for d, kind in deps.items():
            if d is op:
                continue
            if d.dma:
                need.append(d)
            elif d.eng == eng and not dma:
                if eng == "tensor":
                    continue
                if kind == "raw" or self.strict:
                    need.append(d)
            else:
                need.append(d)
        for d in need:
            d.signal = True
        op.deps = need
        for k in r:
            self.readers.setdefault(k, []).append(op)
        for k in w:
            self.last_w[k] = op
            self.readers[k] = []
        if dma:
            c = self.dma_count.get(semkey, 0) + 1
            self.dma_count[semkey] = c
            op.sigval = 16 * c
            if waitall:
                self.waitall.add(semkey)
            if out:
                self.outkeys.add(semkey)
        for rg in regs:
            ro = self.region_ops.setdefault(rg, {})
            ro[("d", semkey) if dma else eng] = op
        self.ops[eng].append(op)
        return op

    def emit(self):
        nc = self.nc
        ctx = self.ctx
        esem = {e: ctx.enter_context(nc.semaphore("s_" + e)) for e in ENGS}
        dsem = {}
        for i, k in enumerate(self.dma_count):
            dsem[k] = ctx.enter_context(nc.semaphore("d%d" % i))
        for e in ENGS:
            c = 0
            for op in self.ops[e]:
                if not op.dma and op.signal:
                    c += 1
                    op.sigval = c
        self.nwaits = 0

        def emit_engine(eng_name):
            def body(eng):
                waited = {}
                for op in self.ops[eng_name]:
                    wl = {}
                    for d in op.deps:
                        if d.dma:
                            s = dsem[d.semkey]
                            v = 16 * self.dma_count[d.semkey] if d.semkey in self.waitall else d.sigval
                        else:
                            s = esem[d.eng]
                            v = d.sigval
                        key = id(s)
                        if v > wl.get(key, (None, 0))[1]:
                            wl[key] = (s, v)
                    for key, (s, v) in wl.items():
                        if waited.get(key, 0) >= v:
                            continue
                        eng.wait_ge(s, v)
                        self.nwaits += 1
                        waited[key] = v
                    ins = op.fn(eng)
                    if op.dma:
                        ins.then_inc(dsem[op.semkey], 16)
                    elif op.signal:
                        ins.then_inc(esem[eng_name], 1)
                if eng_name == "sync":
                    for k in self.outkeys:
                        eng.wait_ge(dsem[k], 16 * self.dma_count[k])
            return body

        with nc.Block() as block:
            block.sync(emit_engine("sync"))
            block.scalar(emit_engine("scalar"))
            block.gpsimd(emit_engine("gpsimd"))
            block.vector(emit_engine("vector"))
            block.tensor(emit_engine("tensor"))


def bc_last(ap, n):
    a = ap.ap
    return bass.AP(ap.tensor, ap.offset, [list(x) for x in a] + [[0, n]])


def bc_mid(ap, n):
    a = ap.ap
    return bass.AP(ap.tensor, ap.offset, [list(a[0]), [0, n]] + [list(x) for x in a[1:]])


class StopBuild(Exception):
    pass


class Builder:
    def __init__(self, debug=(), stop=None):
        self.debug = set(debug)
        self.taps = {}
        self.stop = stop

    def ck(self, name):
        if self.stop == name:
            self.P.frozen = True

    def mm(self, out, lhsT, rhs, start, stop, r, w, skip=False):
        if skip:
            self.P.add("tensor", lambda e: e.matmul(out, lhsT=lhsT, rhs=rhs, start=start, stop=stop, skip_group_check=True), r=r, w=w)
        else:
            self.P.add("tensor", lambda e: e.matmul(out, lhsT=lhsT, rhs=rhs, start=start, stop=stop), r=r, w=w)

    def tr(self, out, in_, r, w):
        ident = self.ident[:]
        self.P.add("tensor", lambda e: e.transpose(out, in_, ident), r=list(r) + ["ident"], w=w)

    def act(self, out, in_, func, r, w, bias=None, scale=1.0, accum=None):
        kw = {}
        if bias is not None:
            kw["bias"] = bias
        if accum is not None:
            kw["accum_out"] = accum
        self.P.add("scalar", lambda e: e.activation(out, in_, func, scale=scale, **kw), r=r, w=w)

    def ts(self, eng, out, in0, s1, s2, op0, op1, r, w):
        if op1 is None:
            self.P.add(eng, lambda e: e.tensor_scalar(out, in0, s1, None, op0), r=r, w=w)
        else:
            self.P.add(eng, lambda e: e.tensor_scalar(out, in0, s1, s2, op0, op1), r=r, w=w)

    def tt(self, eng, out, in0, in1, op, r, w):
        self.P.add(eng, lambda e: e.tensor_tensor(out, in0, in1, op), r=r, w=w)

    def stt(self, eng, out, in0, scalar, in1, op0, op1, r, w):
        self.P.add(eng, lambda e: e.scalar_tensor_tensor(out, in0, scalar, in1, op0, op1), r=r, w=w)

    def cp(self, eng, out, in_, r, w):
        if eng == "scalar":
            self.P.add(eng, lambda e: e.copy(out, in_), r=r, w=w)
        else:
            self.P.add(eng, lambda e: e.tensor_copy(out, in_), r=r, w=w)

    def dma(self, eng, out, in_, r, w, semkey, waitall=False, outflag=False):
        self.P.add(eng, lambda e: e.dma_start(out=out, in_=in_), r=r, w=w, dma=True, semkey=semkey,
                   waitall=waitall, out=outflag)

    def cload(self, out, in_, key, eng="sync"):
        self.dma(eng, out, in_, [], [key], "const", waitall=True)

    def tap(self, name, ap, key):
        if name not in self.debug:
            return
        shape = list(ap.shape)
        d = self.nc.dram_tensor("dbg_" + name, shape, ap.dtype, kind="ExternalOutput").ap()
        self.taps[name] = "dbg_" + name
        self.dma("sync", d, ap, list(key), [], ("dbg", name), outflag=True)

    def view(self, off, nbytes, dt, pattern=None, **kw):
        assert off % 4 == 0 and nbytes % 4 == 0
        v = self.arena[:, off // 4:(off + nbytes) // 4]
        if dt != F32:
            v = v.bitcast(dt)
        if pattern:
            v = v.rearrange(pattern, **kw)
        return v

    def cview(self, ncols, pattern=None, **kw):
        o = self.coff
        self.coff += ncols
        v = self.cst[:, o:o + ncols]
        if pattern:
            v = v.rearrange(pattern, **kw)
        return v

    def wfm_load(self, src):
        i = self.wfm_i % 4
        self.wfm_i += 1
        v = self.wfm[i]
        key = ("wfm", i)
        self.dma("gpsimd", v, src, [], [key], key)
        return v, key

    def wtm_load(self, src, view=None):
        i = self.wtm_i % 2
        self.wtm_i += 1
        v = self.wtm[i] if view is None else view(i)
        key = ("wtm", i)
        self.dma("gpsimd", v, src, [], [key], key)
        return v, key

    def build(self):
        nc = bass.Bass("TRN2", target_bir_lowering=False)
        self.nc = nc
        din = lambda name, shape, dt=F32: nc.dram_tensor(name, shape, dt, kind="ExternalInput").ap()
        D = {}
        D["x"] = din("x", [2, S, 1024])
        D["cT"] = din("cT", [128, 8, 2])
        D["posT"] = din("posT", [128, 2, 16], I32)
        D["w_ada"] = din("w_ada", [48, 128, 8, 128])
        D["b_ada"] = din("b_ada", [128, 48])
        D["n1g"] = din("n1g", [128, 8])
        D["n2g"] = din("n2g", [128, 8])
        D["fing"] = din("fing", [1024])
        D["w_in_fm"] = din("w_in_fm", [32, 128, 8, 128])
        D["w_in_tm"] = din("w_in_tm", [3, 2, 128, 8, 512])
        D["crw"] = din("crw", [128, 8, 4])
        D["crb"] = din("crb", [128, 8])
        D["w_rg"] = din("w_rg", [2, 2, 16, 64, 64])
        D["brg"] = din("brg", [128, 2, 2, 8])
        D["lamrg"] = din("lamrg", [128, 2, 8])
        D["w_rnn_o"] = din("w_rnn_o", [8, 128, 8, 128])
        D["w_attn_o"] = din("w_attn_o", [8, 128, 8, 128])
        D["w_out"] = din("w_out", [2, 128, 8, 512])
        D["lamv"] = din("lamv", [4, 64])
        D["subg"] = din("subg", [128, 1])
        D["w_up"] = din("w_up", [44, 128, 8, 128])
        D["cfw"] = din("cfw", [128, 44, 3])
        D["cfb"] = din("cfb", [128, 44])
        D["w_down"] = din("w_down", [22, 128, 1024])
        self.D = D
        out_d = nc.dram_tensor("out", [2, S, 1024], F32, kind="ExternalOutput").ap()

        with ExitStack() as ctx:
            sb = lambda name, shape, dt: ctx.enter_context(nc.sbuf_tensor(name, shape, dt))
            ARENA = 187904
            self.arena = sb("arena", [128, ARENA // 4], F32)
            self.cst = sb("cst", [128, 1848], F32)
            self.coff = 0
            self.ident = sb("ident", [128, 128], BF16)
            cactb = sb("cactb", [128, 8, 2], BF16)
            rgw = sb("rgw", [128, 2, 2, 8, 128], BF16)
            gb = sb("gb", [128, 1024], F32)
            fingb = sb("fingb", [128, 1024], F32)
            psA = ctx.enter_context(nc.psum_tensor("psA", [128, 2048], F32))
            psB = ctx.enter_context(nc.psum_tensor("psB", [128, 2048], F32))
            P = Prog(nc, ctx)
            self.P = P

            def bank(i):
                t = psA if i < 4 else psB
                j = i % 4
                return t[:, j * 512:(j + 1) * 512]

            def bankbf(i):
                return bank(i).bitcast(BF16)

            pk = lambda i: ("ps", i)

            O_WFM, O_WTM, O_PT, O_SCR = 0, 8192, 24576, 27648
            O_A, O_B, O_C, O_D = 39936, 72704, 105472, 138240
            self.wfm = [self.view(O_WFM + i * 2048, 2048, BF16, "p (k c) -> p k c", c=128) for i in range(4)]
            self.wtm = [self.view(O_WTM + i * 8192, 8192, BF16, "p (k c) -> p k c", c=512) for i in range(2)]
            self.wfm_i = 0
            self.wtm_i = 0
            PT = [self.view(O_PT + i * 1024, 1024, BF16) for i in range(3)]

            def scr(i, dt, pattern=None, **kw):
                return self.view(O_SCR + i * 2048, 2048, dt, pattern, **kw)

            sk = lambda i: ("S", i)
            for nm, rg in [("hT", ["A"]), ("x", ["B", "C"]), ("OT", ["B"]), ("hgT", ["B"]), ("M", ["C"]),
                           ("Dtok", ["C"]), ("qT", ["D"]), ("kT", ["D"]), ("V", ["D"]), ("rgt", ["D"]),
                           ("x1", ["A", "B"]), ("h2T", ["C"]), ("ffn", ["D"])]:
                P.declare(nm, rg)

            hT = self.view(O_A, 32768, BF16, "p (k t) -> p k t", t=S)
            xres = self.view(O_B, 65536, F32, "p (n c) -> p n c", c=1024)
            OT = self.view(O_B, 32768, BF16, "p (k t) -> p k t", t=S)
            hgT = OT
            M = self.view(O_C, 32768, BF16, "p (k t) -> p k t", t=S)
            Dtok = self.view(O_C, 16384, BF16, "p (n c) -> p n c", c=512)
            qT = self.view(O_D, 16384, BF16, "p (h t) -> p h t", t=S)
            kT = self.view(O_D + 16384, 16384, BF16, "p (h t) -> p h t", t=S)
            V = self.view(O_D + 32768, 16640, BF16, "p (n h e) -> p n h e", h=4, e=130)
            x1 = self.view(O_A, 65536, F32, "p (n c) -> p n c", c=1024)
            h2T = self.view(O_C, 32768, BF16, "p (k t) -> p k t", t=S)

            cv = self.cview
            modT = cv(96, "p (k b) -> p k b", b=2)
            bada = cv(48)
            n1g = cv(8)
            n2g = cv(8)
            gs1 = cv(16, "p (k b) -> p k b", b=2)
            gs2 = cv(16, "p (k b) -> p k b", b=2)
            crw = cv(32, "p (k t) -> p k t", t=4)
            crb = cv(8)
            brg = cv(32, "p (g d k) -> p g d k", g=2, d=2)
            bh = cv(32, "p (g d k) -> p g d k", g=2, d=2)
            lamrg = cv(16, "p (d k) -> p d k", d=2)
            sp_ = cv(16, "p (d k) -> p d k", d=2)
            chalf = cv(16, "p (d k) -> p d k", d=2)
            cfull = cv(16, "p (d k) -> p d k", d=2)
            subg = cv(1)
            subgs = cv(1)
            lamq = cv(256, "p (v e) -> p v e", e=64)
            lprod = cv(128, "p (v e) -> p v e", e=64)
            ls = cv(2)
            le = cv(2)
            neglam = cv(1)
            cfw = cv(132, "p (j t) -> p j t", t=3)
            cfb = cv(44)
            ss = cv(16)
            rstd = cv(16)
            ssq = cv(64, "p (n h) -> p n h", h=4)
            rstda = cv(64, "p (n h) -> p n h", h=4)
            posi = cv(32).bitcast(I32).rearrange("p (b n) -> p b n", b=2)
            posf = cv(32, "p (b n) -> p b n", b=2)
            ang = cv(128, "p (n j) -> p n j", j=8)
            kf = cv(128, "p (n j) -> p n j", j=8)
            ki = cv(128).bitcast(I32).rearrange("p (n j) -> p n j", j=8)
            sinT = cv(128, "p (n j) -> p n j", j=8)
            cosT = cv(128, "p (n j) -> p n j", j=8)
            cact32 = cv(16, "p (k b) -> p k b", b=2)
            c1 = cv(1)
            c025 = cv(1)
            ceps = cv(1)
            rec = cv(8)
            gtmp = cv(8)
            ghi32 = cv(8)
            glo32 = cv(8)
            assert self.coff <= 1848, self.coff

            ident = self.ident
            P.add("gpsimd", lambda e: e.memset(ident[:], 0.0), w=["ident"])
            P.add("gpsimd", lambda e: e.affine_select(out=ident[:], in_=ident[:], compare_op=ALU.not_equal, fill=1.0,
                                                      base=0, pattern=[[-1, 128]], channel_multiplier=1),
                  r=["ident"], w=["ident"])
            P.add("gpsimd", lambda e: e.memset(c1, 1.0), w=["c1"])
            P.add("gpsimd", lambda e: e.memset(c025, 0.25), w=["c025"])
            P.add("gpsimd", lambda e: e.memset(ceps, EPS), w=["ceps"])
            rgwk = [("rgw", i) for i in range(8)]
            P.add("gpsimd", lambda e: e.memset(rgw[:], 0.0), w=rgwk)
            self.cload(cact32, D["cT"], "cact32")
            self.cload(posi, D["posT"], "posi")
            self.cload(bada, D["b_ada"], "bada")
            self.cload(n1g, D["n1g"], "n1g")
            self.cload(n2g, D["n2g"], "n2g")
            self.cload(fingb[:], D["fing"].partition_broadcast(128), "fingb")
            self.cload(crw, D["crw"], "crw")
            self.cload(crb, D["crb"], "crb")
            self.cload(brg, D["brg"], "brg")
            self.cload(lamrg, D["lamrg"], "lamrg")
            self.cload(subg, D["subg"], "subg")
            for v_ in range(4):
                self.cload(lamq[:, v_, :], D["lamv"][v_, :].partition_broadcast(128), ("lamq", v_))
            self.cload(cfw, D["cfw"], "cfw")
            self.cload(cfb, D["cfb"], "cfb")
            for g in range(2):
                for d in range(2):
                    for hb in range(2):
                        src = D["w_rg"][g, d].rearrange("(c h) k j -> h k c j", h=2)[hb]
                        dst = rgw[hb * 64:(hb + 1) * 64, g, d, :, hb * 64:(hb + 1) * 64]
                        self.dma("gpsimd", dst, src, [], [("rgw", g * 4 + d * 2 + hb)], "constg", waitall=True)
            self.tap("c_crw", crw, ["crw"])
            self.ck("c1")

            self.act(cactb[:], cact32, AF.Silu, ["cact32"], ["cactb"])
            self.tap("c_cactb", cactb[:], ["cactb"])
            self.tap("c_rgw", rgw[:, 0, 0, 0, :], rgwk)
            self.ck("c2")
            ada_view = lambda i: self.view(O_WTM + i * 8192, 8192, BF16, "p (c k n) -> p c k n", c=4, k=8)
            pend = deque()
            pend.append(self.wtm_load(D["w_ada"][0:4].rearrange("c p k n -> p c k n"), view=ada_view))
            for g4 in range(12):
                wv, wk = pend.popleft()
                if g4 + 1 < 12:
                    pend.append(self.wtm_load(D["w_ada"][4 * (g4 + 1):4 * (g4 + 2)].rearrange("c p k n -> p c k n"), view=ada_view))
                for cl in range(4):
                    ch = 4 * g4 + cl
                    for kc in range(8):
                        self.mm(bank(0)[:, ch * 2:ch * 2 + 2], wv[:, cl, kc, :], cactb[:, kc, :], kc == 0, kc == 7,
                                [wk, "cactb"], [pk(0)])
            self.tt("vector", modT, bank(0)[:, 0:96].rearrange("p (k b) -> p k b", b=2), bc_last(bada, 2), ALU.add,
                    [pk(0), "bada"], ["modT"])
            self.tap("c_modT", modT, ["modT"])
            self.ck("c3")
            for b in range(2):
                self.stt("vector", gs1[:, :, b], modT[:, 8:16, b], 1.0, n1g, ALU.add, ALU.mult, ["modT", "n1g"], [("gs1", b)])
                self.stt("vector", gs2[:, :, b], modT[:, 32:40, b], 1.0, n2g, ALU.add, ALU.mult, ["modT", "n2g"], [("gs2", b)])
            self.act(sp_, lamrg, AF.Exp, ["lamrg"], ["sp"], scale=-1.0)
            self.act(sp_, sp_, AF.Ln, ["sp", "c1"], ["sp"], bias=c1)
            self.ts("vector", chalf, sp_, -4.0, None, ALU.mult, None, ["sp"], ["chalf"])
            self.ts("vector", cfull, sp_, -8.0, None, ALU.mult, None, ["sp"], ["cfull"])
            self.ts("vector", bh, brg, 0.5, None, ALU.mult, None, ["brg"], ["bh"])
            self.tap("c_chalf", chalf, ["chalf"])
            self.ck("c4")
            for i in range(2):
                self.tt("vector", lprod[:, i, :], lamq[:, 2 * i, :], lamq[:, 2 * i + 1, :], ALU.mult,
                        [("lamq", 2 * i), ("lamq", 2 * i + 1)], [("lprod", i)])
                P.add("vector", lambda e, i=i: e.reduce_sum(ls[:, i:i + 1], lprod[:, i, :], axis=AX.X),
                      r=[("lprod", i)], w=[("ls", i)])
            self.act(le, ls, AF.Exp, [("ls", 0), ("ls", 1)], ["le"])
            self.tt("vector", neglam, le[:, 1:2], le[:, 0:1], ALU.subtract, ["le"], ["neglam"])
            self.ts("vector", neglam, neglam, -0.2, None, ALU.add, None, ["neglam"], ["neglam"])
            self.ts("vector", subgs, subg, 0.8, None, ALU.mult, None, ["subg"], ["subgs"])
            self.cp("vector", posf, posi, ["posi"], ["posf"])
            self.tap("modT", modT, ["modT"])

            invf = (np.float32(500000.0) ** (-np.arange(0, 16, 2, dtype=np.float32) / np.float32(16))).astype(np.float32)

            def range_reduce_sin(dst):
                self.ts("vector", kf, ang, 1.0 / (2 * PI), None, ALU.mult, None, ["ang"], ["kf"])
                self.cp("vector", ki, kf, ["kf"], ["ki"])
                self.cp("vector", kf, ki, ["ki"], ["kf"])
                self.stt("vector", ang, kf, -2 * PI, ang, ALU.mult, ALU.add, ["kf", "ang"], ["ang"])
                self.ts("vector", kf, ang, PI, -2 * PI, ALU.is_gt, ALU.mult, ["ang"], ["kf"])
                self.tt("vector", ang, ang, kf, ALU.add, ["ang", "kf"], ["ang"])
                self.ts("vector", kf, ang, -PI, 2 * PI, ALU.is_lt, ALU.mult, ["ang"], ["kf"])
                self.tt("vector", ang, ang, kf, ALU.add, ["ang", "kf"], ["ang"])
                self.act(dst, ang, AF.Sin, ["ang"], [("rope", 0)])

            def gate_row(b, chunk0, scale):
                gcol = modT[:, chunk0:chunk0 + 8, b]
                self.ts("vector", gtmp, gcol, scale, None, ALU.mult, None, ["modT"], ["gtmp"])
                GH = scr(0, BF16, "p (k c) -> p k c", c=128)
                GL = scr(1, BF16, "p (k c) -> p k c", c=128)
                self.cp("vector", GH, bc_last(gtmp, 128), ["gtmp"], [sk(0)])
                self.cp("vector", ghi32, GH[:, :, 0], [sk(0)], ["ghi32"])
                self.tt("vector", glo32, gtmp, ghi32, ALU.subtract, ["gtmp", "ghi32"], ["glo32"])
                self.cp("vector", GL, bc_last(glo32, 128), ["glo32"], [sk(1)])
                for kc in range(8):
                    bk = kc // 4
                    o = bank(bk)[:, (kc % 4) * 128:(kc % 4 + 1) * 128]
                    self.mm(o, GH[:, kc, :], ident[:], True, False, [sk(0), "ident"], [pk(bk)])
                    self.mm(o, GL[:, kc, :], ident[:], False, True, [sk(1), "ident"], [pk(bk)])
                for bk in range(2):
                    self.cp("vector", gb[:, bk * 512:(bk + 1) * 512], bank(bk), [pk(bk)], ["gb"])

            try:
                self.body(locals())
            except StopBuild:
                pass
            P.emit()
        return nc

    def body(self, L):
        globals_ = L
        (P, D, hT, xres, OT, hgT, M, Dtok, qT, kT, V, x1, h2T, modT, gs1, gs2, crw, crb, bh, chalf, cfull, subgs, neglam,
         cfw, cfb, ss, rstd, ssq, rstda, posf, ang, sinT, cosT, c025, ceps, rec, rgw, gb, fingb, psA, psB, bank, bankbf, pk, sk,
         scr, PT, invf, range_reduce_sin, gate_row, out_d, O_SCR, O_D, O_WTM) = [L[k] for k in (
            "P", "D", "hT", "xres", "OT", "hgT", "M", "Dtok", "qT", "kT", "V", "x1", "h2T", "modT", "gs1", "gs2", "crw", "crb",
            "bh", "chalf", "cfull", "subgs", "neglam", "cfw", "cfb", "ss", "rstd", "ssq", "rstda", "posf", "ang", "sinT", "cosT",
            "c025", "ceps", "rec", "rgw", "gb", "fingb", "psA", "psB", "bank", "bankbf", "pk", "sk", "scr", "PT", "invf",
            "range_reduce_sin", "gate_row", "out_d", "O_SCR", "O_D", "O_WTM")]
        if True:
            self.ck("p0")
            for b in range(2):
                for j in range(8):
                    self.ts("vector", ang[:, :, j], posf[:, b, :], float(invf[j]), None, ALU.mult, None, ["posf"], ["ang"])
                range_reduce_sin(sinT)
                for j in range(8):
                    self.ts("vector", ang[:, :, j], posf[:, b, :], float(invf[j]), None, ALU.mult, None, ["posf"], ["ang"])
                self.ts("vector", ang, ang, PI / 2, None, ALU.add, None, ["ang"], ["ang"])
                range_reduce_sin(cosT)

                for rg in ("A", "B", "C"):
                    P.handoff(rg)
                sskeys = [("ss", tt) for tt in range(NT)]
                P.add("vector", lambda e: e.memset(ss, 0.0), w=sskeys)
                for tt in range(NT):
                    self.dma("sync", xres[:, tt, :], D["x"][b, tt * 128:(tt + 1) * 128, :], [], [("x", tt)], ("x", tt))
                    self.act(scr(0, BF16), xres[:, tt, :], AF.Square, [("x", tt)], [sk(0), ("ss", tt)], accum=ss[:, tt:tt + 1])
                sskeys = [("ss", tt) for tt in range(NT)]
                self.ts("vector", rstd, ss, 1.0 / 1024, None, ALU.mult, None, sskeys, ["rstd"])
                self.act(rstd, rstd, AF.Sqrt, ["rstd", "ceps"], ["rstd"], bias=ceps)
                P.add("vector", lambda e: e.reciprocal(rstd, rstd), r=["rstd"], w=["rstd"])
                for tt in range(NT):
                    si = 1 + tt % 2
                    xnb = scr(si, BF16)
                    self.ts("vector", xnb, xres[:, tt, :], rstd[:, tt:tt + 1], None, ALU.mult, None,
                            [("x", tt), "rstd"], [sk(si)])
                    bk = tt % 4
                    pv = bankbf(bk).rearrange("p (k c) -> p k c", c=128)
                    for kc in range(8):
                        self.tr(pv[:, kc, :], xnb[:, kc * 128:(kc + 1) * 128], [sk(si)], [pk(bk)])
                    for kc in range(8):
                        o = hT[:, kc, tt * 128:(tt + 1) * 128]
                        if tt % 2 == 0:
                            self.act(o, pv[:, kc, :], AF.Identity, [pk(bk), ("gs1", b), "modT"], [("hT", tt)],
                                     bias=modT[:, kc, b:b + 1], scale=gs1[:, kc, b:b + 1])
                        else:
                            self.ts("vector", o, pv[:, kc, :], gs1[:, kc, b:b + 1], modT[:, kc, b:b + 1], ALU.mult, ALU.add,
                                    [pk(bk), ("gs1", b), "modT"], [("hT", tt)])
                if b == 0:
                    self.tap("hT", hT, [("hT", tt) for tt in range(NT)])
                    self.ck("p1")
                hTk = lambda tb: [("hT", 4 * tb + i) for i in range(4)]

                for rg in ("B", "C", "D"):
                    P.handoff(rg)
                for hg in range(2):
                    P.add("gpsimd", lambda e: e.memset(V[:, :, :, 128:130], 1.0), w=[("V", tt) for tt in range(NT)])
                    pend = deque()
                    pend.append(self.wtm_load(D["w_in_tm"][0, hg]))
                    for wi in range(3):
                        wv, wk = pend.popleft()
                        if wi + 1 < 3:
                            pend.append(self.wtm_load(D["w_in_tm"][wi + 1, hg]))
                        deferred = []
                        for tt in range(NT):
                            bk = tt % 2
                            for kc in range(8):
                                self.mm(bank(bk), hT[:, kc, tt * 128:(tt + 1) * 128], wv[:, kc, :], kc == 0, kc == 7,
                                        [("hT", tt), wk], [pk(bk)])
                            if wi == 2:
                                self.cp("scalar", V[:, tt, :, 0:128], bank(bk).rearrange("p (h e) -> p h e", e=128),
                                        [pk(bk)], [("V", tt)])
                                continue
                            si = 3 + tt % 2
                            fi = tt % 2
                            qtok = scr(si, BF16)[:, 0:512]
                            q3 = qtok.rearrange("p (g e) -> p g e", e=64)
                            qf = scr(fi, F32)
                            p3 = qf.rearrange("p (g e) -> p g e", e=64)
                            rt = self.view(O_SCR + 5 * 2048, 1024, F32, "p (a g e) -> p a g e", a=4, g=8)
                            self.cp("scalar", qf, bank(bk), [pk(bk)], [sk(fi)])
                            self.cp("vector", qtok, qf, [sk(fi)], [sk(si)])
                            cs = bc_mid(cosT[:, tt, :], 8)
                            sn = bc_mid(sinT[:, tt, :], 8)
                            rk = [sk(fi), ("rope", 0)]
                            self.tt("vector", rt[:, 0], p3[:, :, 0:8], cs, ALU.mult, rk, [("rt", 0)])
                            self.tt("vector", rt[:, 1], p3[:, :, 8:16], sn, ALU.mult, rk, [("rt", 1)])
                            self.tt("vector", q3[:, :, 0:8], rt[:, 0], rt[:, 1], ALU.subtract, [("rt", 0), ("rt", 1)], [sk(si)])
                            self.tt("vector", rt[:, 2], p3[:, :, 8:16], cs, ALU.mult, rk, [("rt", 2)])
                            self.tt("vector", rt[:, 3], p3[:, :, 0:8], sn, ALU.mult, rk, [("rt", 3)])
                            self.tt("vector", q3[:, :, 8:16], rt[:, 2], rt[:, 3], ALU.add, [("rt", 2), ("rt", 3)], [sk(si)])
                            def stage2(tt=tt, wi=wi, qtok=qtok, si=si):
                                tb_ = 2 + tt % 2
                                pv = bankbf(tb_)[:, 0:512].rearrange("p (h c) -> p h c", c=128)
                                for hl in range(4):
                                    self.tr(pv[:, hl, :], qtok[:, hl * 128:(hl + 1) * 128], [sk(si)], [pk(tb_)])
                                dstT = qT if wi == 0 else kT
                                nm = "qT" if wi == 0 else "kT"
                                self.cp("vector", dstT[:, :, tt * 128:(tt + 1) * 128], pv, [pk(tb_)], [(nm, tt)])
                            if deferred:
                                deferred.pop()()
                            deferred.append(stage2)
                        if deferred:
                            deferred.pop()()
                    if b == 0 and hg == 0:
                        self.tap("qT", qT, [("qT", tt) for tt in range(NT)])
                        self.tap("kT", kT, [("kT", tt) for tt in range(NT)])
                        self.tap("V", V, [("V", tt) for tt in range(NT)])
                        self.ck("p2a")
                    LA = 2
                    steps = [(hl, qb, c, kt) for hl in range(4) for qb in range(4) for c in range(2) for kt in range(NT)]
                    NS = len(steps)
                    A0 = scr(0, F32, "p (s e) -> p s e", e=128)
                    T_ = scr(1, F32, "p (s e) -> p s e", e=128)
                    SQ = scr(2, F32, "p (s e) -> p s e", e=128)

                    def emit_qk(i):
                        hl, qb, c, kt = steps[i]
                        sb_ = i % 3
                        qk_ = [("qT", 4 * qb + t_) for t_ in range(4)]
                        self.mm(bank(sb_), kT[c * 64:(c + 1) * 64, hl, kt * 128:(kt + 1) * 128],
                                qT[c * 64:(c + 1) * 64, hl, qb * 512:(qb + 1) * 512], True, True,
                                [("kT", kt)] + qk_, [pk(sb_)])
                        self.act(PT[sb_], bank(sb_), AF.Exp, [pk(sb_)], [("PT", sb_)], scale=0.125)

                    def emit_pv(i):
                        hl, qb, c, kt = steps[i]
                        sb_ = i % 3
                        oacc = psB[:, c * 1024:(c + 1) * 1024].rearrange("p (s w) -> p s w", w=256)
                        ok = [pk(4 + 2 * c), pk(5 + 2 * c)]
                        for s_ in range(4):
                            self.mm(oacc[:, s_, 0:129], PT[sb_][:, s_ * 128:(s_ + 1) * 128], V[:, kt, hl, 0:129],
                                    kt == 0 and s_ % 2 == 0, kt == NT - 1 and s_ % 2 == 1,
                                    [("PT", sb_), ("V", kt)], [ok[s_ // 2]], skip=True)
                        if kt != NT - 1:
                            return
                        rc = rec[:, c * 4:(c + 1) * 4]
                        P.add("vector", lambda e, rc=rc, oacc=oacc: e.reciprocal(rc, oacc[:, :, 128]), r=ok, w=[("rec", c)])
                        if c == 0:
                            self.tt("vector", A0, oacc[:, :, 0:128], bc_last(rc, 128), ALU.mult, ok + [("rec", 0)], [sk(0)])
                        else:
                            self.ts("vector", rc, rc, neglam, None, ALU.mult, None, [("rec", 1), "neglam"], [("rec", 1)])
                            self.tt("vector", T_, oacc[:, :, 0:128], bc_last(rc, 128), ALU.mult, ok + [("rec", 1)], [sk(1)])
                            self.tt("vector", T_, T_, A0, ALU.add, [sk(0), sk(1)], [sk(1)])
                            dk = [("Dtok", 4 * qb + t_) for t_ in range(4)]
                            self.cp("gpsimd", Dtok[:, 4 * qb:4 * qb + 4, hl * 128:(hl + 1) * 128], T_, [sk(1)], dk)
                            self.tt("vector", SQ, T_, T_, ALU.mult, [sk(1)], [sk(2)])
                            P.add("vector", lambda e, qb=qb, hl=hl: e.reduce_sum(ssq[:, 4 * qb:4 * qb + 4, hl], SQ, axis=AX.X),
                                  r=[sk(2)], w=[("ssq", qb, hl)])

                    for i in range(NS + LA):
                        if i < NS:
                            emit_qk(i)
                        if i - LA >= 0:
                            emit_pv(i - LA)
                    sqk = [("ssq", qb, hl) for qb in range(4) for hl in range(4)]
                    self.ts("vector", rstda, ssq, 1.0 / 128, None, ALU.mult, None, sqk, ["rstda"])
                    self.act(rstda, rstda, AF.Sqrt, ["rstda", "ceps"], ["rstda"], bias=ceps)
                    P.add("vector", lambda e: e.reciprocal(rstda, rstda), r=["rstda"], w=["rstda"])
                    deferred = []
                    for tt in range(NT):
                        si = 3 + tt % 2
                        Dn = scr(si, BF16)[:, 0:512]
                        self.tt("vector", Dn.rearrange("p (h e) -> p h e", e=128),
                                Dtok[:, tt, :].rearrange("p (h e) -> p h e", e=128), bc_last(rstda[:, tt, :], 128), ALU.mult,
                                [("Dtok", tt), "rstda"], [sk(si)])
                        bk = tt % 2
                        pv = bankbf(bk)[:, 0:512].rearrange("p (h c) -> p h c", c=128)
                        for hl in range(4):
                            self.tr(pv[:, hl, :], Dn[:, hl * 128:(hl + 1) * 128], [sk(si)], [pk(bk)])

                        def fin2(tt=tt, pv=pv, bk=bk):
                            self.ts("vector", OT[:, hg * 4:(hg + 1) * 4, tt * 128:(tt + 1) * 128], pv, subgs, None, ALU.mult, None,
                                    [pk(bk), "subgs"], [("OT", tt)])
                        if deferred:
                            deferred.pop()()
                        deferred.append(fin2)
                    deferred.pop()()
                if b == 0:
                    self.tap("OT", OT, [("OT", tt) for tt in range(NT)])
                    self.ck("p2")
                OTk = lambda tb: [("OT", 4 * tb + i) for i in range(4)]

                def merge(gate_ch0, wsrc, srcT, srck, first):
                    pend = deque()
                    pend.append((self.wfm_load(D["w_in_fm"][gate_ch0]), self.wfm_load(wsrc[0])))
                    for j in range(8):
                        (wg, wgk), (wo, wok) = pend.popleft()
                        if j + 1 < 8:
                            pend.append((self.wfm_load(D["w_in_fm"][gate_ch0 + j + 1]), self.wfm_load(wsrc[j + 1])))
                        for tb in range(4):
                            bx, by = 2 * (tb % 2), 2 * (tb % 2) + 1
                            cols = slice(tb * 512, (tb + 1) * 512)
                            for kc in range(8):
                                self.mm(bank(bx), wg[:, kc, :], hT[:, kc, cols], kc == 0, kc == 7, [wgk] + hTk(tb), [pk(bx)])
                            for kc in range(8):
                                self.mm(bank(by), wo[:, kc, :], srcT[:, kc, cols], kc == 0, kc == 7, [wok] + srck(tb), [pk(by)])
                            si = tb % 2
                            tg = scr(si, F32)
                            self.act(tg, bank(bx), AF.Tanh, [pk(bx)], [sk(si)], scale=0.5)
                            mk = [("M", j, tb)]
                            if first:
                                self.stt("vector", M[:, j, cols], tg, 1.0, bank(by), ALU.add, ALU.mult, [sk(si), pk(by)], mk)
                            else:
                                mt = scr(2 + si, F32)
                                self.stt("vector", mt, tg, 1.0, bank(by), ALU.add, ALU.mult, [sk(si), pk(by)], [sk(2 + si)])
                                self.tt("gpsimd", M[:, j, cols], mt, M[:, j, cols], ALU.add, [sk(2 + si)] + mk, mk)

                P.handoff("C")
                merge(24, D["w_attn_o"], OT, OTk, True)
                if b == 0:
                    self.ck("p3")

                P.handoff("B")
                P.handoff("D")
                xc32 = self.view(O_D, 8192, F32)
                xcb = self.view(O_D + 8192, 4096, BF16)
                G = self.view(O_D + 12288, 4096, BF16)
                T2 = self.view(O_D + 16384, 8192, F32)
                T3 = self.view(O_D + 24576, 8192, F32)
                Aa = [self.view(O_D + 32768, 8192, F32), self.view(O_D + 40960, 8192, F32)]
                pAk = [pk(i) for i in range(4)]
                pBk = [pk(4 + i) for i in range(4)]
                allh = [("hT", tt) for tt in range(NT)]
                pend = deque()
                pend.append((self.wfm_load(D["w_in_fm"][0]), self.wfm_load(D["w_in_fm"][8])))
                for c in range(8):
                    (wx, wxk), (wy, wyk) = pend.popleft()
                    if c + 1 < 8:
                        pend.append((self.wfm_load(D["w_in_fm"][c + 1]), self.wfm_load(D["w_in_fm"][8 + c + 1])))
                    for kc in range(8):
                        for tb in range(4):
                            self.mm(bank(tb), wx[:, kc, :], hT[:, kc, tb * 512:(tb + 1) * 512], kc == 0, kc == 7,
                                    [wxk] + hTk(tb), [pk(tb)])
                    for kc in range(8):
                        for tb in range(4):
                            self.mm(bank(4 + tb), wy[:, kc, :], hT[:, kc, tb * 512:(tb + 1) * 512], kc == 0, kc == 7,
                                    [wyk] + hTk(tb), [pk(4 + tb)])
                    self.act(xc32, psA[:, :], AF.Identity, pAk + ["crw", "crb"], [("rgt", "xc")], bias=crb[:, c:c + 1], scale=crw[:, c, 2:3])
                    self.stt("vector", xc32[:, 2:S], psA[:, 0:S - 2], crw[:, c, 0:1], xc32[:, 2:S], ALU.mult, ALU.add,
                             pAk + [("rgt", "xc"), "crw"], [("rgt", "xc")])
                    self.stt("vector", xc32[:, 1:S], psA[:, 0:S - 1], crw[:, c, 1:2], xc32[:, 1:S], ALU.mult, ALU.add,
                             pAk + [("rgt", "xc"), "crw"], [("rgt", "xc")])
                    self.stt("vector", xc32[:, 0:S - 1], psA[:, 1:S], crw[:, c, 3:4], xc32[:, 0:S - 1], ALU.mult, ALU.add,
                             pAk + [("rgt", "xc"), "crw"], [("rgt", "xc")])
                    self.cp("gpsimd", xcb, xc32, [("rgt", "xc")], [("rgt", "xcb")])
                    self.act(T2, psB[:, :], AF.Square, pBk, [("rgt", "T2")])
                    self.ts("vector", T2, T2, 0.044715, 1.0, ALU.mult, ALU.add, [("rgt", "T2")], [("rgt", "T2")])
                    self.tt("vector", T2, psB[:, :], T2, ALU.mult, pBk + [("rgt", "T2")], [("rgt", "T2")])
                    self.act(T2, T2, AF.Tanh, [("rgt", "T2")], [("rgt", "T2")], scale=0.7978845608028654)
                    self.stt("vector", G, T2, 1.0, psB[:, :], ALU.add, ALU.mult, pBk + [("rgt", "T2")], [("rgt", "G")])
                    for d in range(2):
                        A_ = Aa[d]
                        ak = ("rgt", "A", d)
                        for tb in range(4):
                            self.mm(bank(tb), rgw[:, 0, d, c, :], xcb[:, tb * 512:(tb + 1) * 512], True, True,
                                    L["rgwk"] + [("rgt", "xcb")], [pk(tb)])
                        for tb in range(4):
                            self.mm(bank(4 + tb), rgw[:, 1, d, c, :], xcb[:, tb * 512:(tb + 1) * 512], True, True,
                                    L["rgwk"] + [("rgt", "xcb")], [pk(4 + tb)])
                        self.act(T2, psA[:, :], AF.Tanh, pAk + ["bh"], [("rgt", "T2")], bias=bh[:, 0, d, c:c + 1], scale=0.5)
                        self.act(T3, psB[:, :], AF.Tanh, pBk + ["bh"], [("rgt", "T3")], bias=bh[:, 1, d, c:c + 1], scale=0.5)
                        self.act(A_, T2, AF.Exp, [("rgt", "T2"), "chalf"], [ak], bias=chalf[:, d, c:c + 1], scale=chalf[:, d, c:c + 1])
                        self.act(T2, T2, AF.Exp, [("rgt", "T2"), "cfull"], [("rgt", "T2")], bias=cfull[:, d, c:c + 1], scale=cfull[:, d, c:c + 1])
                        self.act(T2, T2, AF.Sqrt, [("rgt", "T2"), "c025"], [("rgt", "T2")], bias=c025, scale=-0.25)
                        self.stt("vector", T3, T3, 1.0, xc32, ALU.add, ALU.mult, [("rgt", "T3"), ("rgt", "xc")], [("rgt", "T3")])
                        self.tt("vector", T3, T3, T2, ALU.mult, [("rgt", "T3"), ("rgt", "T2")], [("rgt", "T3")])
                        if d == 0:
                            P.add("vector", lambda e, A_=A_: e.tensor_tensor_scan(A_, A_, T3, 0.0, ALU.mult, ALU.add),
                                  r=[ak, ("rgt", "T3")], w=[ak])
                        else:
                            P.add("vector", lambda e, A_=A_: e.tensor_tensor_scan(A_[:, ::-1], A_[:, ::-1], T3[:, ::-1], 0.0, ALU.mult, ALU.add),
                                  r=[ak, ("rgt", "T3")], w=[ak])
                    self.tt("gpsimd", Aa[0], Aa[0], Aa[1], ALU.add, [("rgt", "A", 0), ("rgt", "A", 1)], [("rgt", "A", 0)])
                    self.stt("vector", hgT[:, c, :], Aa[0], 0.5, G, ALU.mult, ALU.mult, [("rgt", "A", 0), ("rgt", "G")],
                             [("hgT", c)])
                if b == 0:
                    self.tap("hgT", hgT, [("hgT", c) for c in range(8)])
                    self.ck("p4")
                hgk = lambda tb: [("hgT", c) for c in range(8)]

                merge(16, D["w_rnn_o"], hgT, hgk, False)
                if b == 0:
                    self.tap("M", M, [("M", j, tb) for j in range(8) for tb in range(4)])
                    self.ck("p5")

                gate_row(b, 16, 0.5)
                P.handoff("A")
                P.handoff("B")
                P.add("vector", lambda e: e.memset(ss, 0.0), w=sskeys)
                wo_ = []
                for g in range(2):
                    wo_.append(self.wtm_load(D["w_out"][g]))
                for tt in range(NT):
                    self.dma("sync", x1[:, tt, :], D["x"][b, tt * 128:(tt + 1) * 128, :], [], [("x1", tt)], ("x", tt))
                for tt in range(NT):
                    mkeys = [("M", j, tt // 4) for j in range(8)]
                    for g in range(2):
                        bk = (2 * tt + g) % 4
                        wv, wk = wo_[g]
                        for kc in range(8):
                            self.mm(bank(bk), M[:, kc, tt * 128:(tt + 1) * 128], wv[:, kc, :], kc == 0, kc == 7, mkeys + [wk], [pk(bk)])
                        si = (2 * tt + g) % 2
                        tmp = scr(si, F32)
                        cols = slice(g * 512, (g + 1) * 512)
                        self.tt("vector", tmp, bank(bk), gb[:, cols], ALU.mult, [pk(bk), "gb"], [sk(si)])
                        self.tt("gpsimd", x1[:, tt, cols], tmp, x1[:, tt, cols], ALU.add, [sk(si), ("x1", tt)], [("x1", tt)])
                    self.act(scr(2, BF16), x1[:, tt, :], AF.Square, [("x1", tt)], [sk(2), ("ss", tt)], accum=ss[:, tt:tt + 1])
                if b == 0:
                    self.tap("x1", x1, [("x1", tt) for tt in range(NT)])
                    self.ck("p6")

                P.handoff("C")
                P.handoff("D")
                self.ts("vector", rstd, ss, 1.0 / 1024, None, ALU.mult, None, sskeys, ["rstd"])
                self.act(rstd, rstd, AF.Sqrt, ["rstd", "ceps"], ["rstd"], bias=ceps)
                P.add("vector", lambda e: e.reciprocal(rstd, rstd), r=["rstd"], w=["rstd"])
                for tt in range(NT):
                    si = 3 + tt % 2
                    xnb = scr(si, BF16)
                    self.ts("vector", xnb, x1[:, tt, :], rstd[:, tt:tt + 1], None, ALU.mult, None, [("x1", tt), "rstd"], [sk(si)])
                    bk = tt % 4
                    pv = bankbf(bk).rearrange("p (k c) -> p k c", c=128)
                    for kc in range(8):
                        self.tr(pv[:, kc, :], xnb[:, kc * 128:(kc + 1) * 128], [sk(si)], [pk(bk)])
                    for kc in range(8):
                        o = h2T[:, kc, tt * 128:(tt + 1) * 128]
                        if tt % 2 == 0:
                            self.act(o, pv[:, kc, :], AF.Identity, [pk(bk), ("gs2", b), "modT"], [("h2T", tt)],
                                     bias=modT[:, 24 + kc, b:b + 1], scale=gs2[:, kc, b:b + 1])
                        else:
                            self.ts("vector", o, pv[:, kc, :], gs2[:, kc, b:b + 1], modT[:, 24 + kc, b:b + 1], ALU.mult, ALU.add,
                                    [pk(bk), ("gs2", b), "modT"], [("h2T", tt)])
                gate_row(b, 40, 1.0)
                h2k = lambda tb: [("h2T", 4 * tb + i) for i in range(4)]
                aT = self.view(O_D, 24576, BF16, "p (j t) -> p j t", t=S)
                Yv = self.view(O_D + 24576, 8192, F32)
                Yg = self.view(O_D + 32768, 8192, F32)
                wdn = self.view(O_WTM, 16384, BF16, "p (k c) -> p k c", c=1024)
                groups = [(0, 6), (6, 6), (12, 5), (17, 5)]
                for (j0, nj) in groups:
                    self.dma("gpsimd", wdn[:, 0:nj, :], D["w_down"][j0:j0 + nj].rearrange("k p c -> p k c"), [],
                             [("wtm", 0), ("wtm", 1)], ("wtm", 0))
                    pend = deque()
                    pend.append((self.wfm_load(D["w_up"][j0]), self.wfm_load(D["w_up"][22 + j0])))
                    for jl in range(nj):
                        j = j0 + jl
                        (wv_, wvk), (wg_, wgk) = pend.popleft()
                        if jl + 1 < nj:
                            pend.append((self.wfm_load(D["w_up"][j + 1]), self.wfm_load(D["w_up"][22 + j + 1])))
                        for kc in range(8):
                            for tb in range(4):
                                self.mm(bank(tb), wv_[:, kc, :], h2T[:, kc, tb * 512:(tb + 1) * 512], kc == 0, kc == 7,
                                        [wvk] + h2k(tb), [pk(tb)])
                        for kc in range(8):
                            for tb in range(4):
                                self.mm(bank(4 + tb), wg_[:, kc, :], h2T[:, kc, tb * 512:(tb + 1) * 512], kc == 0, kc == 7,
                                        [wgk] + h2k(tb), [pk(4 + tb)])
                        for (Y, ps_, pks, jj, yk) in ((Yv, psA, pAk, j, ("ffn", "Yv")), (Yg, psB, pBk, 22 + j, ("ffn", "Yg"))):
                            self.act(Y, ps_[:, :], AF.Identity, pks + ["cfw", "cfb"], [yk], bias=cfb[:, jj:jj + 1], scale=cfw[:, jj, 1:2])
                            self.stt("vector", Y[:, 1:S], ps_[:, 0:S - 1], cfw[:, jj, 0:1], Y[:, 1:S], ALU.mult, ALU.add,
                                     pks + [yk, "cfw"], [yk])
                            self.stt("vector", Y[:, 0:S - 1], ps_[:, 1:S], cfw[:, jj, 2:3], Y[:, 0:S - 1], ALU.mult, ALU.add,
                                     pks + [yk, "cfw"], [yk])
                        self.act(Yg, Yg, AF.Silu, [("ffn", "Yg")], [("ffn", "Yg")])
                        self.tt("vector", aT[:, jl, :], Yg, Yv, ALU.mult, [("ffn", "Yg"), ("ffn", "Yv")], [("ffn", "aT", jl)])
                    if b == 0 and j0 == 0:
                        self.tap("aT", aT, [("ffn", "aT", jl) for jl in range(6)])
                        self.ck("p7")
                    ak_ = [("ffn", "aT", jl) for jl in range(nj)]
                    for tt in range(NT):
                        for g in range(2):
                            bk = (2 * tt + g) % 4
                            cols = slice(g * 512, (g + 1) * 512)
                            for jl in range(nj):
                                self.mm(bank(bk), aT[:, jl, tt * 128:(tt + 1) * 128], wdn[:, jl, cols], jl == 0, jl == nj - 1,
                                        ak_ + [("wtm", 0), ("wtm", 1)], [pk(bk)])
                            si = (2 * tt + g) % 2
                            tmp = scr(si, F32)
                            self.tt("vector", tmp, bank(bk), gb[:, cols], ALU.mult, [pk(bk), "gb"], [sk(si)])
                            self.tt("gpsimd", x1[:, tt, cols], tmp, x1[:, tt, cols], ALU.add, [sk(si), ("x1", tt)], [("x1", tt)])
                P.add("vector", lambda e: e.memset(ss, 0.0), w=sskeys)
                for tt in range(NT):
                    self.act(scr(2, BF16), x1[:, tt, :], AF.Square, [("x1", tt)], [sk(2), ("ss", tt)], accum=ss[:, tt:tt + 1])
                self.ts("vector", rstd, ss, 1.0 / 1024, None, ALU.mult, None, sskeys, ["rstd"])
                self.act(rstd, rstd, AF.Sqrt, ["rstd", "ceps"], ["rstd"], bias=ceps)
                P.add("vector", lambda e: e.reciprocal(rstd, rstd), r=["rstd"], w=["rstd"])
                for tt in range(NT):
                    self.stt("vector", x1[:, tt, :], x1[:, tt, :], rstd[:, tt:tt + 1], fingb[:], ALU.mult, ALU.mult,
                             [("x1", tt), "rstd", "fingb"], [("x1", tt)])
                    self.dma("sync", out_d[b, tt * 128:(tt + 1) * 128, :], x1[:, tt, :], [("x1", tt)], [], ("x", tt), outflag=True)


def fm_layout(w, nch):
    return np.ascontiguousarray(w.reshape(8, 128, nch, 128).transpose(2, 1, 0, 3))


def tm_layout(w, ng):
    return np.ascontiguousarray(w.reshape(8, 128, ng, 512).transpose(2, 1, 0, 3))


def pcol(v, n):
    return np.ascontiguousarray(v.reshape(n, 128).T)


_NC_CACHE = {}


def prepare_shared(inp):
    sh = {}
    sh["w_ada"] = fm_layout(inp["w_ada"][0], 48)
    sh["b_ada"] = pcol(inp["b_ada"][0], 48)
    sh["n1g"] = pcol(inp["norm1_g"][0], 8)
    sh["n2g"] = pcol(inp["norm2_g"][0], 8)
    sh["fing"] = np.ascontiguousarray(inp["final_g"])
    w_in = inp["w_in"][0]
    sh["w_in_fm"] = np.concatenate([fm_layout(w_in[:, 0:2048], 16), fm_layout(w_in[:, 5120:7168], 16)], axis=0)
    sh["w_in_tm"] = np.stack([tm_layout(w_in[:, 2048:3072], 2), tm_layout(w_in[:, 3072:4096], 2),
                              tm_layout(w_in[:, 4096:5120], 2)], axis=0)
    sh["crw"] = np.ascontiguousarray(inp["conv_rnn_w"][0].reshape(4, 8, 128).transpose(2, 1, 0))
    sh["crb"] = pcol(inp["conv_rnn_b"][0], 8)
    sh["w_rg"] = np.ascontiguousarray(np.stack([inp["w_rg_a"][0], inp["w_rg_i"][0]], axis=0))
    brg = np.stack([inp["b_rg_a"][0], inp["b_rg_i"][0]], axis=0)
    sh["brg"] = np.ascontiguousarray(brg.reshape(2, 2, 8, 128).transpose(3, 0, 1, 2))
    sh["lamrg"] = np.ascontiguousarray(inp["rg_lambda"][0].reshape(2, 8, 128).transpose(2, 0, 1))
    sh["w_rnn_o"] = fm_layout(inp["w_rnn_o"][0], 8)
    sh["w_attn_o"] = fm_layout(inp["w_attn_o"][0], 8)
    sh["w_out"] = tm_layout(inp["w_out"][0], 2)
    sh["lamv"] = np.ascontiguousarray(np.stack([inp["lam_q1"][0], inp["lam_k1"][0], inp["lam_q2"][0], inp["lam_k2"][0]], axis=0))
    sh["subg"] = np.ascontiguousarray(inp["subln_g"][0].reshape(128, 1))
    sh["w_up"] = fm_layout(inp["w_up"][0], 44)
    sh["cfw"] = np.ascontiguousarray(inp["conv_ffn_w"][0].reshape(3, 44, 128).transpose(2, 1, 0))
    sh["cfb"] = pcol(inp["conv_ffn_b"][0], 44)
    sh["w_down"] = np.ascontiguousarray(inp["w_down"][0].reshape(22, 128, 1024))
    return {k: np.ascontiguousarray(v, dtype=np.float32) for k, v in sh.items()}


def make_in_maps(inp, n_cores=8):
    sh = prepare_shared(inp)
    maps = []
    for i in range(n_cores):
        m = dict(sh)
        m["x"] = np.ascontiguousarray(inp["x"][2 * i:2 * i + 2], dtype=np.float32)
        c2 = inp["c"][2 * i:2 * i + 2]
        m["cT"] = np.ascontiguousarray(c2.reshape(2, 8, 128).transpose(2, 1, 0), dtype=np.float32)
        p2 = inp["positions"][2 * i:2 * i + 2]
        m["posT"] = np.ascontiguousarray(p2.reshape(2, 16, 128).transpose(2, 0, 1), dtype=np.int32)
        maps.append(m)
    return maps


def kernel(**inputs):
    inp = {k: np.asarray(v) for k, v in inputs.items()}
    if "nc" not in _NC_CACHE:
        _NC_CACHE["nc"] = Builder().build()
    nc = _NC_CACHE["nc"]
    maps = make_in_maps(inp, 8)
    res = run_bass_kernel_spmd(nc, maps, core_ids=list(range(8)))
    out = np.concatenate([r["out"] for r in res.results], axis=0)
    return out.astype(np.float32)
```
